# Optimizing a Trainium2 kernel written in Bass

```python
import jax, jax.numpy as jnp
from jax import lax
import numpy as np

D_MODEL = 2048
BATCH = 2
SEQ = 4096
DEPTH = 1

GRID_W = 64
PLE_DIM = 256
D_ATTN = D_MODEL // 2
D_CONV = D_MODEL - D_ATTN
N_HEADS = 8
HEAD_DIM = D_ATTN // N_HEADS
N_CONV_GROUPS = 8
CONV_W = 3
WIN_ROWS_MAX = 8
WIN_COLS = 16
D_FF = 5632
RMS_EPS = 1e-6
D_IN = 3 * D_ATTN + 3 * D_CONV

kernel_name = 'hybrid_natten_shortconv_macaron_block'


def _rmsnorm(x, g):
    x32 = x.astype(jnp.float32)
    y = x32 * lax.rsqrt(jnp.mean(x32 * x32, axis=-1, keepdims=True) + RMS_EPS)
    return (y * g.astype(jnp.float32)).astype(x.dtype)


def _group_rmsnorm(x, g, n_groups):
    b, s, c = x.shape
    xg = x.reshape(b, s, n_groups, c // n_groups).astype(jnp.float32)
    y = xg * lax.rsqrt(jnp.mean(xg * xg, axis=-1, keepdims=True) + RMS_EPS)
    return (y.reshape(b, s, c) * g.astype(jnp.float32)).astype(x.dtype)


def _swiglu(x, wg, wu, wd):
    return (jax.nn.silu(x @ wg) * (x @ wu)) @ wd


def _neighbourhood_attention(q, k, v, rpb):
    b, s, h, dh = q.shape
    rows = s // GRID_W
    kr = min(WIN_ROWS_MAX, rows)
    r = jnp.arange(rows)
    row_start = jnp.clip(r - kr // 2, 0, rows - kr)
    row_idx = row_start[:, None] + jnp.arange(kr)[None, :]
    c = jnp.arange(GRID_W)
    col_start = jnp.clip(c - WIN_COLS // 2, 0, GRID_W - WIN_COLS)
    in_win = (c[None, :] >= col_start[:, None]) & (c[None, :] < col_start[:, None] + WIN_COLS)

    qg = q.reshape(b, rows, GRID_W, h, dh)
    kg = k.reshape(b, rows, GRID_W, h, dh)[:, row_idx]
    vg = v.reshape(b, rows, GRID_W, h, dh)[:, row_idx]

    scores = jnp.einsum('brqhd,brkwhd->bhrqkw', qg, kg).astype(jnp.float32) * (dh ** -0.5)

    rel_r = row_idx - r[:, None]
    rel_c = jnp.clip(c[None, :] - c[:, None], -(WIN_COLS - 1), WIN_COLS - 1)
    bias = rpb[:, rel_r[:, None, :, None] + (WIN_ROWS_MAX - 1),
               rel_c[None, :, None, :] + (WIN_COLS - 1)]
    scores = scores + bias.astype(jnp.float32)[None]
    scores = jnp.where(in_win[:, None, :], scores, -1e30)
    probs = jax.nn.softmax(scores, axis=(-2, -1))
    out = jnp.einsum('bhrqkw,brkwhd->brqhd', probs.astype(v.dtype), vg)
    return out.reshape(b, s, h * dh)


def _short_conv(u, w, bias):
    s = u.shape[1]
    pad = CONV_W // 2
    up = jnp.pad(u, ((0, 0), (pad, CONV_W - 1 - pad), (0, 0)))
    y = up[:, 0:s] * w[0]
    for j in range(1, CONV_W):
        y = y + up[:, j:j + s] * w[j]
    return y + bias


def setup_inputs(seed: int = 0) -> dict:
    key = jax.random.key(seed)
    ks = jax.random.split(key, 24)
    f32 = jnp.float32

    def nrm(k, shape, scale):
        return jax.random.normal(k, shape, f32) * scale

    def gain(k, shape):
        return 1.0 + 0.02 * jax.random.normal(k, shape, f32)

    L, D = DEPTH, D_MODEL
    return {
        'x': nrm(ks[0], (BATCH, SEQ, D), 1.0),
        'p': nrm(ks[1], (DEPTH, BATCH, SEQ, PLE_DIM), 1.0),
        'ffn1_norm': gain(ks[2], (L, D)),
        'ffn1_wg': nrm(ks[3], (L, D, D_FF), D ** -0.5),
        'ffn1_wu': nrm(ks[4], (L, D, D_FF), D ** -0.5),
        'ffn1_wd': nrm(ks[5], (L, D_FF, D), D_FF ** -0.5),
        'mix_norm': gain(ks[6], (L, D)),
        'w_in': nrm(ks[7], (L, D, D_IN), D ** -0.5),
        'rpb': nrm(ks[8], (L, N_HEADS, 2 * WIN_ROWS_MAX - 1, 2 * WIN_COLS - 1), 0.1),
        'conv_w': nrm(ks[9], (L, CONV_W, D_CONV), CONV_W ** -0.5),
        'conv_b': nrm(ks[10], (L, D_CONV), 0.01),
        'attn_out_norm': gain(ks[11], (L, D_ATTN)),
        'conv_out_norm': gain(ks[12], (L, D_CONV)),
        'w_out': nrm(ks[13], (L, D_ATTN + D_CONV, D), (D_ATTN + D_CONV) ** -0.5),
        'ffn2_norm': gain(ks[14], (L, D)),
        'ffn2_wg': nrm(ks[15], (L, D, D_FF), D ** -0.5),
        'ffn2_wu': nrm(ks[16], (L, D, D_FF), D ** -0.5),
        'ffn2_wd': nrm(ks[17], (L, D_FF, D), D_FF ** -0.5),
        'ple_norm': gain(ks[18], (L, D)),
        'ple_w_gate': nrm(ks[19], (L, D, D), D ** -0.5),
        'ple_w_proj': nrm(ks[20], (L, PLE_DIM, D), PLE_DIM ** -0.5),
        'final_norm': gain(ks[21], (D,)),
    }


def reference(x, p, ffn1_norm, ffn1_wg, ffn1_wu, ffn1_wd, mix_norm, w_in, rpb, conv_w, conv_b,
              attn_out_norm, conv_out_norm, w_out, ffn2_norm, ffn2_wg, ffn2_wu, ffn2_wd,
              ple_norm, ple_w_gate, ple_w_proj, final_norm):
    b, s, _ = x.shape
    splits = [D_ATTN, 2 * D_ATTN, 3 * D_ATTN, 3 * D_ATTN + D_CONV, 3 * D_ATTN + 2 * D_CONV]
    h = x
    for i in range(DEPTH):
        h = h + 0.5 * _swiglu(_rmsnorm(h, ffn1_norm[i]), ffn1_wg[i], ffn1_wu[i], ffn1_wd[i])

        a = _rmsnorm(h, mix_norm[i])
        z = a @ w_in[i]
        q, k, v, gate_b, gate_c, u = jnp.split(z, splits, axis=-1)
        q = q.reshape(b, s, N_HEADS, HEAD_DIM)
        k = k.reshape(b, s, N_HEADS, HEAD_DIM)
        v = v.reshape(b, s, N_HEADS, HEAD_DIM)
        y_attn = _neighbourhood_attention(q, k, v, rpb[i])
        y_conv = gate_b * _short_conv(gate_c * u, conv_w[i], conv_b[i])
        mixed = jnp.concatenate([
            _group_rmsnorm(y_attn, attn_out_norm[i], N_HEADS),
            _group_rmsnorm(y_conv, conv_out_norm[i], N_CONV_GROUPS)], axis=-1)
        h = h + mixed @ w_out[i]

        h = h + 0.5 * _swiglu(_rmsnorm(h, ffn2_norm[i]), ffn2_wg[i], ffn2_wu[i], ffn2_wd[i])

        g = jax.nn.sigmoid(_rmsnorm(h, ple_norm[i]) @ ple_w_gate[i])
        h = h + g * (p[i] @ ple_w_proj[i])
    return _rmsnorm(h, final_norm)
```

```python
import numpy as np
import concourse.bass as bass
import concourse.mybir as mybir
from concourse.bass_utils import run_bass_kernel_spmd

F32 = mybir.dt.float32
BF16 = mybir.dt.bfloat16
AF = mybir.ActivationFunctionType
ALU = mybir.AluOpType
AX = mybir.AxisListType

D = 2048
NK = 16
DFF = 5632
NF = 44
OWN = 1024
HALO = 448
EPS = 1e-6
NB = 8
SAME_ENG_SYNC = True

G_FFN1, G_MIX, G_FFN2, G_PLE, G_FIN = 0, 16, 32, 48, 64
G_ATT, G_CONV, G_CW0, G_CW1, G_CW2, G_CB = 80, 88, 96, 104, 112, 120

T0 = ("own", 0, 512)
T1 = ("own", 512, 512)
T2 = ("halo", 0, 448)


class Op:
    __slots__ = ("eng", "fn", "deps", "dma", "milestone", "seq", "waits")

    def __init__(self, eng, fn, deps, dma):
        self.eng, self.fn, self.deps, self.dma = eng, fn, deps, dma
        self.milestone = False
        self.seq = 0
        self.waits = []


class Prog:
    COMPUTE = ("pe", "act", "dve")

    def __init__(self):
        self.ops = []
        self.last_w = {}
        self.readers = {}
        self.dma_counts = {}
        self.last_on = {}
        self.rec = None

    def begin_region(self):
        assert self.rec is None
        self.rec = []

    def end_region(self, schedule=True, pess=1.3, dma_serial=False):
        items = self.rec
        self.rec = None
        n = len(items)
        order = list(range(n))
        if schedule and n > 1:
            last_w, readers = {}, {}
            deps = [set() for _ in range(n)]
            for i, (eng, fn, r, w, dma, cost, prio) in enumerate(items):
                for k in r:
                    j = last_w.get(k)
                    if j is not None:
                        deps[i].add(j)
                for k in w:
                    j = last_w.get(k)
                    if j is not None:
                        deps[i].add(j)
                    for j in readers.get(k, ()):
                        deps[i].add(j)
                ws = set(w)
                for k in w:
                    last_w[k] = i
                    readers[k] = []
                for k in r:
                    if k not in ws:
                        readers.setdefault(k, []).append(i)
                deps[i].discard(i)
            succ = [[] for _ in range(n)]
            indeg = [0] * n
            for i in range(n):
                indeg[i] = len(deps[i])
                for j in deps[i]:
                    succ[j].append(i)
            ready_t = [0.0] * n
            finish = [0.0] * n
            eng_free = {}
            ready = [i for i in range(n) if indeg[i] == 0]
            order = []
            while ready:
                best = None
                for i in ready:
                    eng, fn, r, w, dma, cost, prio = items[i]
                    st = max(eng_free.get(eng, 0.0), ready_t[i])
                    key = (st, prio, i)
                    if best is None or key < best[0]:
                        best = (key, i)
                (st, _, _), i = best
                ready.remove(i)
                eng, fn, r, w, dma, cost, prio = items[i]
                c = cost if cost is not None else 0.3
                if eng != "pe" and dma is None:
                    c = c * pess
                if dma is not None:
                    eng_free[eng] = st + (c if dma_serial else 0.3 * c)
                    finish[i] = st + c + (1.0 if dma_serial else 2.0)
                else:
                    eng_free[eng] = st + c
                    finish[i] = st + c
                order.append(i)
                for j in succ[i]:
                    lat = (0.1 if (items[j][0] == eng and dma is None) else 0.2) * (pess if eng != "pe" else 1.0)
                    ready_t[j] = max(ready_t[j], finish[i] + lat)
                    indeg[j] -= 1
                    if indeg[j] == 0:
                        ready.append(j)
            assert len(order) == n
            self.last_makespan = max(finish) if n else 0.0
        for i in order:
            eng, fn, r, w, dma, cost, prio = items[i]
            self.add(eng, fn, r=r, w=w, dma=dma)

    def add(self, eng, fn, r=(), w=(), dma=None, extra=(), cost=None, prio=1):
        if self.rec is not None:
            assert not extra
            self.rec.append((eng, fn, tuple(r), tuple(w), dma, cost, prio))
            return None
        i = len(self.ops)
        deps = set(extra)
        for k in r:
            j = self.last_w.get(k)
            if j is not None:
                deps.add(j)
        for k in w:
            j = self.last_w.get(k)
            if j is not None:
                deps.add(j)
            for j in self.readers.get(k, ()):
                deps.add(j)
        dm = None
        if dma is not None:
            self.dma_counts[dma] = self.dma_counts.get(dma, 0) + 16
            dm = (dma, self.dma_counts[dma])
        op = Op(eng, fn, deps, dm)
        self.ops.append(op)
        wset = set(w)
        for k in w:
            self.last_w[k] = i
            self.readers[k] = []
        for k in r:
            if k in wset:
                continue
            lst = self.readers.setdefault(k, [])
            if dm is None:
                lst[:] = [j for j in lst if not (self.ops[j].dma is None and self.ops[j].eng == eng)]
            lst.append(i)
        if fn is not None:
            self.last_on[eng] = i
        return i

    def barrier(self):
        lasts = [self.last_on[e] for e in self.COMPUTE if e in self.last_on]
        for e in ("pe", "act", "dve", "pool", "sp"):
            self.add(e, None, extra=lasts)

    def _skip(self, prod, cons):
        if prod.dma is not None:
            return False
        if prod.eng != cons.eng:
            return False
        if prod.eng == "pe":
            return True
        return not SAME_ENG_SYNC

    def resolve(self):
        ops = self.ops
        for op in ops:
            for j in op.deps:
                pj = ops[j]
                if pj.dma is None and not self._skip(pj, op):
                    pj.milestone = True
        cnt = {}
        for op in ops:
            if op.dma is None and op.milestone:
                assert op.fn is not None
                cnt[op.eng] = cnt.get(op.eng, 0) + 1
                op.seq = cnt[op.eng]
        seen = {}
        for op in ops:
            waits = {}
            for j in op.deps:
                pj = ops[j]
                if pj.dma is not None:
                    key, val = ("dma", pj.dma[0]), pj.dma[1]
                else:
                    if self._skip(pj, op):
                        continue
                    key, val = ("eng", pj.eng), pj.seq
                if waits.get(key, 0) < val:
                    waits[key] = val
            sn = seen.setdefault(op.eng, {})
            op.waits = []
            for k, v in waits.items():
                if sn.get(k, 0) < v:
                    op.waits.append((k, v))
                    sn[k] = v


def _geom(qb):
    if qb == 0:
        lo, n = 0, 6
    elif qb == 7:
        lo, n = 6, 6
    else:
        lo, n = qb, 5
    nk = sum(64 if b == 11 else 128 for b in range(lo, lo + n))
    return lo, n, nk


def _tbl_index(qb):
    return {0: 0, 1: 1, 6: 3, 7: 4}.get(qb, 2)


def build_program(debug=False):
    nc = bass.Bass("TRN2", target_bir_lowering=False)

    def dram_in(name, shape):
        return nc.dram_tensor(name, list(shape), F32, kind="ExternalInput").ap()

    xo = dram_in("xo", [D, OWN])
    xh = dram_in("xh", [D, HALO])
    pT = dram_in("pT", [256, OWN])
    gains_d = dram_in("gains", [128, 128])
    ident_d = dram_in("ident", [128, 128])
    tbl_d = dram_in("tbl", [5, 8, 128, 768])
    w = {}
    for nm, shp in (("ffn1_wg", [D, DFF]), ("ffn1_wu", [D, DFF]), ("ffn1_wd", [DFF, D]),
                    ("w_in", [D, 6144]), ("w_out", [D, D]),
                    ("ffn2_wg", [D, DFF]), ("ffn2_wu", [D, DFF]), ("ffn2_wd", [DFF, D]),
                    ("ple_w_gate", [D, D]), ("ple_w_proj", [256, D])):
        w[nm] = dram_in(nm, shp)
    outT = nc.dram_tensor("outT", [D, OWN], F32, kind="ExternalOutput").ap()
    dbg = {}
    if debug:
        for nm, shp in (("dbg_h1", [D, OWN]), ("dbg_mixed", [D, OWN]), ("dbg_h2", [D, OWN]),
                        ("dbg_h3", [D, OWN])):
            dbg[nm] = nc.dram_tensor(nm, shp, F32, kind="ExternalOutput").ap()

    P = Prog()
    ARENA_F32 = 53200
    with (
        nc.sbuf_tensor("arena", [128, ARENA_F32], F32) as ar,
        nc.psum_tensor("ps", [128, 4096], F32) as ps,
    ):
        def cv(off, nbytes, dtype):
            assert off % 4 == 0 and nbytes % 4 == 0
            assert off + nbytes <= ARENA_F32 * 4, (off, nbytes)
            v = ar[:, off // 4:(off + nbytes) // 4]
            return v.bitcast(BF16) if dtype is BF16 else v

        off = 0

        def take(nbytes):
            nonlocal off
            o = off
            off += nbytes
            return o

        h_own = cv(take(65536), 65536, F32).rearrange("p (k t) -> p k t", k=NK)
        a_own = cv(take(32768), 32768, BF16).rearrange("p (k t) -> p k t", k=NK)
        a_halo = cv(take(14336), 14336, BF16).rearrange("p (k t) -> p k t", k=NK)
        gains = cv(take(512), 512, F32)
        ident = cv(take(256), 256, BF16)
        ones = cv(take(256), 256, BF16)
        identf = cv(take(512), 512, F32)
        ring = [cv(take(4096), 4096, BF16) for _ in range(NB)]
        LOC = off
        LOC_SIZE = ARENA_F32 * 4 - LOC

        def loc(o, nbytes, dtype):
            assert o + nbytes <= LOC_SIZE, (o, nbytes, LOC_SIZE)
            return cv(LOC + o, nbytes, dtype)

        h_halo = loc(0, 28672, F32).rearrange("p (k t) -> p k t", k=NK)
        NT = 28672
        sq = [loc(NT + i * 1024, 1024, BF16) for i in range(2)]
        lnt = loc(NT + 2048, 2048, F32)
        rstd = [loc(NT + 4096 + i * 2048, 2048, F32) for i in range(2)]
        HID1 = NT + 8192
        sq2 = [loc(i * 1024, 1024, BF16) for i in range(2)]
        lnt2 = loc(2048, 2048, F32)
        rstd2 = [loc(4096 + i * 2048, 2048, F32) for i in range(2)]
        HID2 = 8192

        def bank(b, n=512, c0=0):
            return ps[:, b * 512 + c0: b * 512 + c0 + n]

        def h_ap(k, t):
            reg, s, n = t
            return (h_own if reg == "own" else h_halo)[:, k, s:s + n]

        def a_ap(k, t):
            reg, s, n = t
            return (a_own if reg == "own" else a_halo)[:, k, s:s + n]

        def tkey(t):
            return 2 if t[0] == "halo" else t[1] // 512

        pieces = []
        state = {"next_dma": 0, "next_use": 0}

        def plan_col(W, c0):
            pieces.append(("col", W.rearrange("(k p) f -> p k f", p=128)[:, :, c0:c0 + 128]))

        def plan_row(W, r0):
            pieces.append(("row", W[r0:r0 + 128, :]))

        def emit_dma(i, extra_r=()):
            kind, src = pieces[i]
            slot = i % NB
            dst = ring[slot].rearrange("p (k f) -> p k f", k=NK) if kind == "col" else ring[slot]
            P.add("pool", lambda e, dst=dst, src=src: e.dma_start(out=dst, in_=src),
                  r=list(extra_r), w=[("ring", slot)], dma=("ring", slot), cost=5.5)

        def use_piece():
            i = state["next_use"]
            state["next_use"] += 1
            assert i < state["next_dma"], "weight piece used before its DMA was emitted"
            kind, _ = pieces[i]
            slot = i % NB
            view = ring[slot].rearrange("p (k f) -> p k f", k=NK) if kind == "col" else ring[slot]
            return i, view, ("ring", slot)

        def release(i):
            j = state["next_dma"]
            if j < len(pieces):
                assert j == i + NB, (i, j)
                emit_dma(j)
                state["next_dma"] += 1

        def plan_ffn(wg, wu, wd, GC):
            ng = NF // GC
            for g in range(ng + 1):
                if g < ng:
                    for gi in range(GC):
                        fc = g * GC + gi
                        plan_col(wg, fc * 128)
                        plan_col(wu, fc * 128)
                if g >= 1:
                    for gi in range(GC):
                        fc = (g - 1) * GC + gi
                        plan_row(wd, fc * 128)

        GC1, GC2 = 4, 4
        plan_ffn(w["ffn1_wg"], w["ffn1_wu"], w["ffn1_wd"], GC1)
        for hh in range(8):
            for base in (0, 1024, 2048):
                plan_col(w["w_in"], base + hh * 128)
        for c in range(8):
            for base in (5120, 4096, 3072):
                plan_col(w["w_in"], base + c * 128)
        for dc in range(NK):
            plan_col(w["w_out"], dc * 128)
        plan_ffn(w["ffn2_wg"], w["ffn2_wu"], w["ffn2_wd"], GC2)
        for dc in range(NK):
            plan_col(w["ple_w_gate"], dc * 128)

        P.begin_region()
        P.add("sp", lambda e: e.dma_start(out=gains, in_=gains_d), w=[("gains",)], dma=("misc", 0), cost=0.5)
        P.add("sp", lambda e: e.dma_start(out=identf, in_=ident_d), w=[("identf",)], dma=("misc", 1), cost=0.5)
        xo_v = xo.rearrange("(k p) t -> p k t", p=128)
        xh_v = xh.rearrange("(k p) t -> p k t", p=128)
        P.add("sp", lambda e: e.dma_start(out=h_own[:, 0:8, 0:512], in_=xo_v[:, 0:8, 0:512]),
              w=[("h", k, 0) for k in range(8)], dma=("x", 0), cost=6.0)
        P.add("sp", lambda e: e.dma_start(out=h_own[:, 8:16, 0:512], in_=xo_v[:, 8:16, 0:512]),
              w=[("h", k, 0) for k in range(8, NK)], dma=("x", 3), cost=6.0)
        P.add("sp", lambda e: e.dma_start(out=h_own[:, :, 512:1024], in_=xo_v[:, :, 512:1024]),
              w=[("h", k, 1) for k in range(NK)], dma=("x", 1), cost=14.0)
        P.add("sp", lambda e: e.dma_start(out=h_halo[:, :, :], in_=xh_v),
              w=[("h", k, 2) for k in range(NK)], dma=("x", 2), cost=12.0)
        for i in range(NB):
            emit_dma(i, extra_r=([("h", 15, 0)] if i >= 2 else []) + ([("h", 15, 1)] if i >= 4 else []))
        state["next_dma"] = NB
        P.add("dve", lambda e: e.memset(ones, 1.0), w=[("ones",)], cost=0.2)
        P.add("dve", lambda e: e.tensor_copy(out=ident, in_=identf), r=[("identf",)], w=[("ident",)], cost=0.3)

        def norm(tiles, gcol, sqb, lntb, rstdb, out_fn=None, out_keys=None):
            for ti, t in enumerate(tiles):
                n = t[2]
                tk = tkey(t)
                for k in range(NK):
                    s_ = sqb[k % 2][:, :n]
                    P.add("act", lambda e, s_=s_, k=k, t=t: e.activation(out=s_, in_=h_ap(k, t), func=AF.Square),
                          r=[("h", k, tk)], w=[("sq", k % 2)], cost=n / 1000.0 + 0.1, prio=0)
                    P.add("pe", lambda e, s_=s_, k=k, n=n: e.matmul(bank(7, n), lhsT=ones, rhs=s_,
                                                                   start=(k == 0), stop=(k == NK - 1)),
                          r=[("sq", k % 2), ("ones",)], w=[("ps", 7)], cost=n / 2400.0 + 0.02, prio=0)
                rs = rstdb[ti % 2][:, :n]
                P.add("act", lambda e, n=n: e.activation(out=lntb[:, :n], in_=bank(7, n), func=AF.Ln,
                                                         scale=1.0 / D, bias=EPS),
                      r=[("ps", 7)], w=[("lnt",)], cost=n / 1000.0 + 0.2, prio=0)
                P.add("act", lambda e, n=n, rs=rs: e.activation(out=rs, in_=lntb[:, :n], func=AF.Exp, scale=-0.5),
                      r=[("lnt",)], w=[("rstd", ti % 2)], cost=n / 1000.0 + 0.2, prio=0)
                for k in range(NK):
                    if out_fn is None:
                        o_ap, okeys = a_ap(k, t), [("a", k, tk)]
                    else:
                        o_ap, okeys = out_fn(k, t), out_keys(k, t)
                    P.add("dve", lambda e, o_ap=o_ap, k=k, t=t, rs=rs: e.scalar_tensor_tensor(
                        out=o_ap, in0=h_ap(k, t), scalar=gains[:, gcol + k:gcol + k + 1], in1=rs,
                        op0=ALU.mult, op1=ALU.mult),
                        r=[("h", k, tk), ("rstd", ti % 2), ("gains",)], w=okeys, cost=n / 960.0 + 0.07, prio=0)

        def ffn(tiles, GC, hid_off, pre=None, post=None, region_open=False, serial_dma=False):
            ng = NF // GC
            ntok = sum(t[2] for t in tiles)
            toff = []
            o = 0
            for t in tiles:
                toff.append(o)
                o += t[2]
            hid = [loc(hid_off + s * GC * ntok * 2, GC * ntok * 2, BF16).rearrange("p (g t) -> p g t", g=GC)
                   for s in range(2)]
            silu_off = hid_off + 2 * GC * ntok * 2
            silu = [loc(silu_off + i * 2048, 2048, F32) for i in range(2)]
            cnt = {"gu": 0, "dn": 0}

            def gate_up(g):
                for gi in range(GC):
                    ig, vg, kg = use_piece()
                    iu, vu, ku = use_piece()
                    for ti, t in enumerate(tiles):
                        n = t[2]
                        tk = tkey(t)
                        c = cnt["gu"]
                        cnt["gu"] += 1
                        bA, bB = c % 2, 2 + c % 2

                        def mm(e, wv, b, t=t, n=n):
                            for k in range(NK):
                                ins = e.matmul(bank(b, n), lhsT=wv[:, k, :], rhs=a_ap(k, t),
                                               start=(k == 0), stop=(k == NK - 1))
                            return ins
                        P.add("pe", lambda e, vg=vg, bA=bA, mm=mm: mm(e, vg, bA),
                              r=[kg] + [("a", k, tk) for k in range(NK)], w=[("ps", bA)], cost=16 * n / 2400.0 + 0.05)
                        P.add("pe", lambda e, vu=vu, bB=bB, mm=mm: mm(e, vu, bB),
                              r=[ku] + [("a", k, tk) for k in range(NK)], w=[("ps", bB)], cost=16 * n / 2400.0 + 0.05)
                        sl = silu[c % 2][:, :n]
                        P.add("act", lambda e, sl=sl, bA=bA, n=n: e.activation(out=sl, in_=bank(bA, n), func=AF.Silu),
                              r=[("ps", bA)], w=[("silu", c % 2)], cost=n / 1000.0 + 0.1)
                        hv = hid[g % 2][:, gi, toff[ti]:toff[ti] + n]
                        P.add("dve", lambda e, hv=hv, sl=sl, bB=bB, n=n: e.tensor_tensor(
                            out=hv, in0=bank(bB, n), in1=sl, op=ALU.mult),
                            r=[("ps", bB), ("silu", c % 2)], w=[("hid", g % 2, gi, ti)], cost=n / 960.0 + 0.07)
                    release(ig)
                    release(iu)

            def down(g):
                pcs = [use_piece() for _ in range(GC)]
                for ti, t in enumerate(tiles):
                    n = t[2]
                    tk = tkey(t)
                    for dc in range(NK):
                        c = cnt["dn"]
                        cnt["dn"] += 1
                        b = 4 + c % 3

                        def mm(e, b=b, n=n, dc=dc, ti=ti):
                            for gi in range(GC):
                                ins = e.matmul(bank(b, n), lhsT=pcs[gi][1][:, dc * 128:(dc + 1) * 128],
                                               rhs=hid[g % 2][:, gi, toff[ti]:toff[ti] + n],
                                               start=(gi == 0), stop=(gi == GC - 1))
                            return ins
                        P.add("pe", mm, r=[p_[2] for p_ in pcs] + [("hid", g % 2, gi, ti) for gi in range(GC)],
                              w=[("ps", b)], cost=GC * n / 2400.0 + 0.03)
                        P.add("dve", lambda e, b=b, n=n, dc=dc, t=t: e.scalar_tensor_tensor(
                            out=h_ap(dc, t), in0=bank(b, n), scalar=0.5, in1=h_ap(dc, t),
                            op0=ALU.mult, op1=ALU.add),
                            r=[("ps", b), ("h", dc, tk)], w=[("h", dc, tk)], cost=n / 960.0 + 0.07)
                for p_ in pcs:
                    release(p_[0])

            for g in range(ng + 1):
                if g == 0:
                    if not region_open:
                        P.begin_region()
                    if pre is not None:
                        pre()
                    gate_up(0)
                    P.end_region(dma_serial=serial_dma)
                elif g == ng:
                    P.begin_region()
                    down(g - 1)
                    if post is not None:
                        post()
                    P.end_region()
                else:
                    gate_up(g)
                    down(g - 1)

        tiles_ext = [T0, T1, T2]
        tiles_own = [T0, T1]

        mixed = loc(0, 32768, BF16).rearrange("p (k t) -> p k t", k=NK)
        AT = 32768
        QO = AT + 16512
        qT = [loc(QO + s * 2048, 2048, BF16) for s in range(2)]
        kT = [loc(QO + 4096 + s * 2944, 2944, BF16) for s in range(2)]
        vv = [loc(QO + 9984 + s * 3168, 3168, BF16).rearrange("p (b d) -> p b d", b=12) for s in range(2)]
        assert QO >= HID1 + 2 * GC1 * 1472 and QO + 16320 <= LOC_SIZE
        s3 = [loc(AT + i * 3072, 3072, F32) for i in range(3)]
        p_bf = [loc(AT + 9216 + i * 1536, 1536, BF16) for i in range(2)]
        pT_sb = [loc(AT + 12288 + i * 1536, 1536, BF16) for i in range(2)]
        yb = [loc(AT + 15360 + i * 256, 256, BF16) for i in range(2)]
        junk = [loc(AT + 15872 + i * 256, 256, BF16) for i in range(2)]
        stt = [loc(AT + 16384 + i * 64, 64, F32) for i in range(2)]
        assert AT + 16512 <= LOC_SIZE
        SB = (2, 5)
        MB = (4, 7)

        def psT(par):
            return ps[:, MB[par] * 512: MB[par] * 512 + 384].bitcast(BF16)

        def psY(par):
            return ps[:, MB[par] * 512 + 384: MB[par] * 512 + 448].bitcast(BF16)

        def v_ones():
            for s in range(2):
                P.add("dve", lambda e, s=s: e.memset(vv[s][:, :, 128:129], 1.0), r=[("a", 15, 2)],
                      w=[("v", s, j) for j in range(3)], cost=0.2)

        att_scale = 128.0 ** -0.5
        pcnt = {"p": 0, "nb": 2}

        def pbank():
            c = pcnt["p"]
            pcnt["p"] += 1
            return c % pcnt["nb"]

        def blk_cols(b):
            if b < 2:
                return ("halo", b * 128, 128)
            if b < 10:
                return ("own", (b - 2) * 128, 128)
            if b == 10:
                return ("halo", 256, 128)
            return ("halo", 384, 64)

        def emit_tbl(gi):
            hh, qb = divmod(gi, 8)
            lo, nkb, nk = _geom(qb)
            src = tbl_d[_tbl_index(qb), hh, :, 0:nk]
            dst = s3[gi % 3][:, 0:nk]
            P.add("sp", lambda e, dst=dst, src=src: e.dma_start(out=dst, in_=src),
                  w=[("s3", gi % 3)], dma=("tbl", gi % 3), cost=1.5, prio=0)

        def proj_mm(wv, wkey, b, n, rhs_fn, tks, out_ap=None):
            for q4 in range(4):
                def mm(e, q4=q4):
                    for k in range(q4 * 4, q4 * 4 + 4):
                        ins = e.matmul(out_ap if out_ap is not None else bank(b, n), lhsT=wv[:, k, :], rhs=rhs_fn(k),
                                       start=(k == 0), stop=(k == NK - 1))
                    return ins
                P.add("pe", mm, r=[wkey] + [("a", k, tk) for k in range(q4 * 4, q4 * 4 + 4) for tk in tks],
                      w=[("ps", b)], cost=4 * max(n, 128) / 2400.0 + 0.02, prio=1)

        def proj(hh):
            s = hh % 2
            iq, vq, kq = use_piece()
            for t in tiles_own:
                b = pbank()
                proj_mm(vq, kq, b, 512, lambda k, t=t: a_ap(k, t), [tkey(t)])
                P.add("act", lambda e, b=b, t=t, s=s: e.mul(qT[s][:, t[1]:t[1] + 512], bank(b), att_scale),
                      r=[("ps", b)], w=[("q", s, t[1] // 512)], cost=0.6, prio=1)
            release(iq)
            ik, vk, kk_ = use_piece()
            for t in tiles_ext:
                b = pbank()
                n = t[2]
                proj_mm(vk, kk_, b, n, lambda k, t=t: a_ap(k, t), [tkey(t)])
                if t[0] == "own":
                    d0 = 256 + t[1]
                    P.add("dve", lambda e, b=b, d0=d0, s=s: e.tensor_copy(out=kT[s][:, d0:d0 + 512], in_=bank(b)),
                          r=[("ps", b)], w=[("k", s, 1 + t[1] // 512)], cost=0.6, prio=1)
                else:
                    P.add("dve", lambda e, b=b, s=s: e.tensor_copy(out=kT[s][:, 0:256], in_=bank(b, 256)),
                          r=[("ps", b)], w=[("k", s, 0)], cost=0.33, prio=1)
                    P.add("dve", lambda e, b=b, s=s: e.tensor_copy(out=kT[s][:, 1280:1472], in_=bank(b, 192, 256)),
                          r=[("ps", b)], w=[("k", s, 3)], cost=0.26, prio=1)
            release(ik)
            iv, vvw, kv = use_piece()
            for j in range(3):
                b = pbank()
                for bb in range(4):
                    blk = j * 4 + bb
                    reg, c0, ntok = blk_cols(blk)
                    asrc = a_own if reg == "own" else a_halo
                    tk = 2 if reg == "halo" else c0 // 512

                    def mm(e, asrc=asrc, c0=c0, ntok=ntok, bb=bb, vvw=vvw, b=b):
                        for k in range(NK):
                            ins = e.matmul(ps[0:ntok, b * 512 + bb * 128: b * 512 + (bb + 1) * 128],
                                           lhsT=asrc[:, k, c0:c0 + ntok], rhs=vvw[:, k, :],
                                           start=(k == 0), stop=(k == NK - 1))
                        return ins
                    P.add("pe", mm, r=[kv] + [("a", k, tk) for k in range(NK)], w=[("ps", b)], cost=0.95, prio=1)
                if j < 2:
                    P.add("act", lambda e, j=j, s=s, b=b: e.copy(
                        vv[s][:, j * 4:(j + 1) * 4, 0:128],
                        bank(b).rearrange("p (b d) -> p b d", b=4)),
                        r=[("ps", b)], w=[("v", s, j)], cost=0.6, prio=1)
                else:
                    def cpv(e, s=s, b=b):
                        e.copy(vv[s][:, 8:11, 0:128], bank(b, 384).rearrange("p (b d) -> p b d", b=3))
                        return e.copy(vv[s][0:64, 11, 0:128], ps[0:64, b * 512 + 384: b * 512 + 512])
                    P.add("act", cpv, r=[("ps", b)], w=[("v", s, j)], cost=0.8, prio=1)
            release(iv)

        def kkeys(s, lo, nkb):
            c0, c1 = lo * 128, min((lo + nkb) * 128, 1472)
            out = []
            for i, (a0, a1) in enumerate(((0, 256), (256, 768), (768, 1280), (1280, 1472))):
                if c0 < a1 and c1 > a0:
                    out.append(("k", s, i))
            return out

        def attn_block(gi):
            hh, qb = divmod(gi, 8)
            s = hh % 2
            par = gi % 2
            lo, nkb, nk = _geom(qb)
            k0 = lo * 128
            n1 = min(nk, 512)
            n2 = nk - n1
            sb = SB[par]
            mb = MB[par]
            sc = s3[gi % 3]
            st = stt[par]
            emit_tbl(gi)

            def mm_s(e):
                ins = e.matmul(ps[:, sb * 512: sb * 512 + n1], lhsT=qT[s][:, qb * 128:(qb + 1) * 128],
                               rhs=kT[s][:, k0:k0 + n1], start=True, stop=True)
                if n2 > 0:
                    ins = e.matmul(ps[:, (sb + 1) * 512: (sb + 1) * 512 + n2], lhsT=qT[s][:, qb * 128:(qb + 1) * 128],
                                   rhs=kT[s][:, k0 + n1:k0 + nk], start=True, stop=True)
                return ins
            P.add("pe", mm_s, r=[("q", s, qb // 4)] + kkeys(s, lo, nkb), w=[("ps", sb), ("ps", sb + 1)],
                  cost=0.32, prio=0)
            P.add("dve", lambda e: e.tensor_tensor(out=sc[:, :nk], in0=ps[:, sb * 512: sb * 512 + nk],
                                                   in1=sc[:, :nk], op=ALU.add),
                  r=[("ps", sb), ("ps", sb + 1), ("s3", gi % 3)], w=[("s3", gi % 3)], cost=nk / 960.0 + 0.1, prio=0)
            P.add("dve", lambda e: e.tensor_reduce(out=st[:, 0:1], in_=sc[:, :nk], axis=AX.X,
                                                   op=ALU.max, negate=True),
                  r=[("s3", gi % 3)], w=[("st", par, 0)], cost=nk / 960.0 + 0.1, prio=0)
            P.add("act", lambda e: e.activation(out=p_bf[par][:, :nk], in_=sc[:, :nk], func=AF.Exp,
                                                bias=st[:, 0:1], scale=1.0),
                  r=[("s3", gi % 3), ("st", par, 0)], w=[("p", par)], cost=nk / 1000.0 + 0.05, prio=0)

            def mm_t(e):
                for jb in range(nkb):
                    wdt = 64 if lo + jb == 11 else 128
                    ins = e.transpose(out=psT(par)[0:wdt, jb * 128:(jb + 1) * 128],
                                      in_=p_bf[par][:, jb * 128: jb * 128 + wdt], identity=ident)
                return ins
            P.add("pe", mm_t, r=[("p", par), ("ident",)], w=[("ps", mb)], cost=nkb * 0.058 + 0.15, prio=0)
            if lo + nkb - 1 == 11:
                nf = (nkb - 1) * 128

                def cp(e):
                    e.copy(pT_sb[par][:, :nf], psT(par)[:, :nf])
                    return e.copy(pT_sb[par][0:64, nf:nf + 128], psT(par)[0:64, nf:nf + 128])
                P.add("act", cp, r=[("ps", mb)], w=[("pT", par)], cost=nkb * 0.1 + 0.3, prio=0)
            else:
                P.add("act", lambda e: e.copy(pT_sb[par][:, :nkb * 128], psT(par)[:, :nkb * 128]),
                      r=[("ps", mb)], w=[("pT", par)], cost=nkb * 0.1 + 0.08, prio=0)

            def mm_o(e):
                for jb in range(nkb):
                    kk = 64 if lo + jb == 11 else 128
                    ins = e.matmul(bank(mb, 129), lhsT=pT_sb[par][0:kk, jb * 128:(jb + 1) * 128],
                                   rhs=vv[s][0:kk, lo + jb, 0:129], start=(jb == 0), stop=(jb == nkb - 1))
                return ins
            vks = sorted(set(("v", s, (lo + jb) // 4) for jb in range(nkb)))
            P.add("pe", mm_o, r=[("pT", par)] + vks, w=[("ps", mb)], cost=nkb * 0.058 + 0.15, prio=0)
            P.add("dve", lambda e: e.reciprocal(out=st[:, 1:2], in_=ps[:, mb * 512 + 128: mb * 512 + 129]),
                  r=[("ps", mb)], w=[("st", par, 1)], cost=0.08, prio=0)
            P.add("act", lambda e: e.activation(out=junk[par], in_=bank(mb, 128), func=AF.Square,
                                                scale=st[:, 1:2], accum_out=st[:, 2:3]),
                  r=[("ps", mb), ("st", par, 1)], w=[("st", par, 2), ("junk", par)], cost=0.36, prio=0)
            P.add("act", lambda e: e.activation(out=st[:, 3:4], in_=st[:, 2:3], func=AF.Ln,
                                                scale=1.0 / 128, bias=EPS),
                  r=[("st", par, 2)], w=[("st", par, 3)], cost=0.2, prio=0)
            P.add("act", lambda e: e.activation(out=st[:, 4:5], in_=st[:, 3:4], func=AF.Exp, scale=-0.5),
                  r=[("st", par, 3)], w=[("st", par, 4)], cost=0.2, prio=0)
            P.add("dve", lambda e: e.tensor_tensor(out=st[:, 5:6], in0=st[:, 4:5], in1=st[:, 1:2], op=ALU.mult),
                  r=[("st", par, 4), ("st", par, 1)], w=[("st", par, 5)], cost=0.16, prio=0)
            P.add("dve", lambda e: e.tensor_scalar(out=yb[par], in0=bank(mb, 128), scalar1=st[:, 5:6], scalar2=None,
                                                   op0=ALU.mult),
                  r=[("ps", mb), ("st", par, 5)], w=[("yb", par)], cost=0.35, prio=0)
            P.add("pe", lambda e: e.transpose(out=psY(par), in_=yb[par], identity=ident),
                  r=[("yb", par), ("ident",)], w=[("ps", mb)], cost=0.2, prio=0)
            P.add("act", lambda e: e.mul(mixed[:, hh, qb * 128:(qb + 1) * 128], psY(par),
                                         gains[:, G_ATT + hh:G_ATT + hh + 1]),
                  r=[("ps", mb), ("gains",)], w=[("mixed", hh, qb // 4)], cost=0.3, prio=0)

        def ffn1_post():
            norm(tiles_ext, G_MIX, sq, lnt, rstd)
            v_ones()
            proj(0)

        ffn(tiles_ext, GC1, HID1, pre=lambda: norm(tiles_ext, G_FFN1, sq, lnt, rstd),
            post=ffn1_post, region_open=True, serial_dma=True)
        if debug:
            P.add("sp", lambda e: e.dma_start(out=dbg["dbg_h1"].rearrange("(k p) t -> p k t", p=128), in_=h_own[:, :, :]),
                  r=[("h", k, tt) for k in range(NK) for tt in (0, 1)], dma=("dbg", 0))
        P.barrier()

        P.begin_region()
        proj(1)
        for hh in range(8):
            for qb in range(8):
                attn_block(hh * 8 + qb)
            if hh + 2 < 8:
                proj(hh + 2)
        P.end_region()
        print("[sched] heads region est makespan us", getattr(P, "last_makespan", None))

        P.barrier()
        CO = 32768
        u_sb = [loc(CO + i * 4104, 4104, F32) for i in range(2)]
        m_sb = [loc(CO + 8208 + i * 4104, 4104, F32) for i in range(2)]
        y_sb = [loc(CO + 16416 + i * 2048, 2048, F32) for i in range(2)]
        yc_sb = [loc(CO + 20512 + i * 2048, 2048, F32) for i in range(2)]
        sqc = loc(CO + 24608, 1024, BF16)
        lntc = loc(CO + 25632, 2048, F32)
        rstdc = loc(CO + 27680, 2048, F32)
        assert CO + 29728 <= LOC_SIZE
        segs = [("own", 0, 512, 1), ("own", 512, 512, 513), ("edge", 255, 2, None)]
        pcnt["nb"] = 7
        P.begin_region()
        for c in range(8):
            cp = c % 2
            us, ms = u_sb[cp], m_sb[cp]
            iu, vu, ku = use_piece()
            ic, vc, kc = use_piece()
            ib, vb, kb = use_piece()
            for (reg, c0, n, m0) in segs:
                b = pbank()
                asrc = a_own if reg == "own" else a_halo
                tks = [c0 // 512] if reg == "own" else [2]
                proj_mm(vu, ku, b, n, lambda k, asrc=asrc, c0=c0, n=n: asrc[:, k, c0:c0 + n], tks)
                if reg == "own":
                    P.add("act", lambda e, b=b, m0=m0, us=us: e.copy(us[:, m0:m0 + 512], bank(b)),
                          r=[("ps", b)], w=[("u_sb", cp, m0)], cost=0.6)
                else:
                    P.add("act", lambda e, b=b, us=us: e.copy(us[:, 0:1], bank(b, 1, 0)),
                          r=[("ps", b)], w=[("u_sb", cp, 0)], cost=0.2)
                    P.add("act", lambda e, b=b, us=us: e.copy(us[:, 1025:1026], bank(b, 1, 1)),
                          r=[("ps", b)], w=[("u_sb", cp, 1025)], cost=0.2)
            release(iu)
            for (reg, c0, n, m0) in segs:
                b = pbank()
                asrc = a_own if reg == "own" else a_halo
                tks = [c0 // 512] if reg == "own" else [2]
                proj_mm(vc, kc, b, n, lambda k, asrc=asrc, c0=c0, n=n: asrc[:, k, c0:c0 + n], tks)
                if reg == "own":
                    P.add("dve", lambda e, b=b, m0=m0, us=us, ms=ms: e.tensor_tensor(
                        out=ms[:, m0:m0 + 512], in0=bank(b), in1=us[:, m0:m0 + 512], op=ALU.mult),
                        r=[("ps", b), ("u_sb", cp, m0)], w=[("m", cp, m0)], cost=0.63)
                else:
                    P.add("dve", lambda e, b=b, us=us, ms=ms: e.tensor_tensor(
                        out=ms[:, 0:1], in0=bank(b, 1, 0), in1=us[:, 0:1], op=ALU.mult),
                        r=[("ps", b), ("u_sb", cp, 0)], w=[("m", cp, 0)], cost=0.1)
                    P.add("dve", lambda e, b=b, us=us, ms=ms: e.tensor_tensor(
                        out=ms[:, 1025:1026], in0=bank(b, 1, 1), in1=us[:, 1025:1026], op=ALU.mult),
                        r=[("ps", b), ("u_sb", cp, 1025)], w=[("m", cp, 1025)], cost=0.1)
            release(ic)
            mkeys = [("m", cp, 0), ("m", cp, 1), ("m", cp, 513), ("m", cp, 1025)]
            for ti, t in enumerate(tiles_own):
                b = pbank()
                o = t[1]
                ys, ycs = y_sb[ti], yc_sb[ti]
                proj_mm(vb, kb, b, 512, lambda k, t=t: a_ap(k, t), [tkey(t)])
                P.add("dve", lambda e, o=o, c=c, ms=ms, ys=ys: e.tensor_scalar(
                    out=ys, in0=ms[:, o:o + 512], scalar1=gains[:, G_CW0 + c:G_CW0 + c + 1], scalar2=None,
                    op0=ALU.mult), r=mkeys + [("gains",)], w=[("y", ti)], cost=0.6)
                P.add("dve", lambda e, o=o, c=c, ms=ms, ys=ys: e.scalar_tensor_tensor(
                    out=ys, in0=ms[:, o + 1:o + 513], scalar=gains[:, G_CW1 + c:G_CW1 + c + 1], in1=ys,
                    op0=ALU.mult, op1=ALU.add), r=mkeys + [("y", ti)], w=[("y", ti)], cost=0.6)
                P.add("dve", lambda e, o=o, c=c, ms=ms, ys=ys: e.scalar_tensor_tensor(
                    out=ys, in0=ms[:, o + 2:o + 514], scalar=gains[:, G_CW2 + c:G_CW2 + c + 1], in1=ys,
                    op0=ALU.mult, op1=ALU.add), r=mkeys + [("y", ti)], w=[("y", ti)], cost=0.6)
                P.add("dve", lambda e, b=b, c=c, ys=ys, ycs=ycs: e.scalar_tensor_tensor(
                    out=ycs, in0=ys, scalar=gains[:, G_CB + c:G_CB + c + 1], in1=bank(b),
                    op0=ALU.add, op1=ALU.mult), r=[("y", ti), ("ps", b)], w=[("yc", ti)], cost=0.63)
                P.add("act", lambda e, ycs=ycs: e.activation(out=sqc, in_=ycs, func=AF.Square),
                      r=[("yc", ti)], w=[("sqc",)], cost=0.6)
                P.add("pe", lambda e: e.matmul(bank(7), lhsT=ones, rhs=sqc, start=True, stop=True),
                      r=[("sqc",), ("ones",)], w=[("ps", 7)], cost=0.25)
                P.add("act", lambda e: e.activation(out=lntc, in_=bank(7), func=AF.Ln, scale=1.0 / 128, bias=EPS),
                      r=[("ps", 7)], w=[("lntc",)], cost=0.7)
                P.add("act", lambda e: e.activation(out=rstdc, in_=lntc, func=AF.Exp, scale=-0.5),
                      r=[("lntc",)], w=[("rstdc",)], cost=0.7)
                P.add("dve", lambda e, o=o, c=c, ycs=ycs: e.scalar_tensor_tensor(
                    out=mixed[:, 8 + c, o:o + 512], in0=ycs, scalar=gains[:, G_CONV + c:G_CONV + c + 1],
                    in1=rstdc, op0=ALU.mult, op1=ALU.mult),
                    r=[("yc", ti), ("rstdc",), ("gains",)], w=[("mixed", 8 + c, ti)], cost=0.6)
            release(ib)

        if debug:
            P.add("pool", lambda e: e.dma_start(out=dbg["dbg_mixed"].rearrange("(k p) t -> p k t", p=128), in_=mixed[:, :, :]),
                  r=[("mixed", mc, tt) for mc in range(NK) for tt in (0, 1)], dma=("dbg", 1))

        for dc in range(NK):
            io, vo, ko = use_piece()
            for ti, t in enumerate(tiles_own):
                b = pbank()
                o = t[1]
                for q4 in range(4):
                    def mm(e, b=b, o=o, vo=vo, q4=q4):
                        for mc in range(q4 * 4, q4 * 4 + 4):
                            ins = e.matmul(bank(b), lhsT=vo[:, mc, :], rhs=mixed[:, mc, o:o + 512],
                                           start=(mc == 0), stop=(mc == NK - 1))
                        return ins
                    P.add("pe", mm, r=[ko] + [("mixed", mc, ti) for mc in range(q4 * 4, q4 * 4 + 4)], w=[("ps", b)],
                          cost=0.88)
                P.add("dve", lambda e, b=b, dc=dc, t=t: e.tensor_tensor(out=h_ap(dc, t), in0=bank(b), in1=h_ap(dc, t),
                                                                        op=ALU.add),
                      r=[("ps", b), ("h", dc, ti)], w=[("h", dc, ti)], cost=0.63)
            release(io)
        sq3 = [loc(CO + i * 1024, 1024, BF16) for i in range(2)]
        lnt3 = loc(CO + 2048, 2048, F32)
        rstd3 = [loc(CO + 4096 + i * 2048, 2048, F32) for i in range(2)]
        norm(tiles_own, G_FFN2, sq3, lnt3, rstd3)

        HID2N = 40960
        pt_bf = loc(8192, 4096, BF16).rearrange("p (k t) -> p k t", k=2)
        wple = loc(12288, 8192, BF16).rearrange("p (k f) -> p k f", k=2)
        sig = [loc(20480 + i * 2048, 2048, F32) for i in range(2)]
        P.add("pool", lambda e: e.dma_start(out=pt_bf, in_=pT.rearrange("(k p) t -> p k t", p=128)),
              r=[("a", 15, 0), ("a", 15, 1)], w=[("pt",)], dma=("misc", 2), cost=3.0)
        P.add("pool", lambda e: e.dma_start(out=wple, in_=w["ple_w_proj"].rearrange("(k p) f -> p k f", p=128)),
              r=[("a", 15, 0), ("a", 15, 1)], w=[("wple",)], dma=("misc", 3), cost=4.0)
        ffn(tiles_own, GC2, HID2N, pre=None,
            post=lambda: norm(tiles_own, G_PLE, sq2, lnt2, rstd2), region_open=True)
        print("[sched] conv+wout+ffn2-head region est makespan us", getattr(P, "last_makespan", None))
        if debug:
            P.add("sp", lambda e: e.dma_start(out=dbg["dbg_h2"].rearrange("(k p) t -> p k t", p=128), in_=h_own[:, :, :]),
                  r=[("h", k, tt) for k in range(NK) for tt in (0, 1)], dma=("dbg", 2))
        if debug:
            P.add("sp", lambda e: e.dma_start(out=dbg["dbg_h3"].rearrange("(k p) t -> p k t", p=128), in_=h_own[:, :, :]),
                  r=[("h", k, tt) for k in range(NK) for tt in (0, 1)], dma=("dbg", 3))

        P.begin_region()
        cc = 0
        for dc in range(NK):
            ig, vg, kg = use_piece()
            for ti, t in enumerate(tiles_own):
                o = t[1]
                bA, bB = cc % 2, 2 + cc % 2
                si = cc % 2
                cc += 1
                for q4 in range(4):
                    def mm(e, bA=bA, t=t, vg=vg, q4=q4):
                        for k in range(q4 * 4, q4 * 4 + 4):
                            ins = e.matmul(bank(bA), lhsT=vg[:, k, :], rhs=a_ap(k, t), start=(k == 0), stop=(k == NK - 1))
                        return ins
                    P.add("pe", mm, r=[kg] + [("a", k, ti) for k in range(q4 * 4, q4 * 4 + 4)], w=[("ps", bA)], cost=0.88)

                def mm2(e, bB=bB, o=o, dc=dc):
                    for k2 in range(2):
                        ins = e.matmul(bank(bB), lhsT=wple[:, k2, dc * 128:(dc + 1) * 128], rhs=pt_bf[:, k2, o:o + 512],
                                       start=(k2 == 0), stop=(k2 == 1))
                    return ins
                P.add("pe", mm2, r=[("wple",), ("pt",)], w=[("ps", bB)], cost=0.45)
                P.add("act", lambda e, bA=bA, si=si: e.activation(out=sig[si], in_=bank(bA), func=AF.Sigmoid),
                      r=[("ps", bA)], w=[("sig", si)], cost=0.62)
                P.add("dve", lambda e, bB=bB, si=si: e.tensor_tensor(out=sig[si], in0=bank(bB), in1=sig[si], op=ALU.mult),
                      r=[("ps", bB), ("sig", si)], w=[("sig", si)], cost=0.62)
                P.add("dve", lambda e, si=si, dc=dc, t=t: e.tensor_tensor(out=h_ap(dc, t), in0=sig[si], in1=h_ap(dc, t),
                                                                         op=ALU.add),
                      r=[("sig", si), ("h", dc, ti)], w=[("h", dc, ti)], cost=0.62)
            release(ig)
        norm(tiles_own, G_FIN, sq2, lnt2, rstd2, out_fn=lambda k, t: h_ap(k, t),
             out_keys=lambda k, t: [("h", k, tkey(t))])
        outv = outT.rearrange("(k p) t -> p k t", p=128)
        for ti, t in enumerate(tiles_own):
            o = t[1]
            for kh in range(2):
                P.add("sp", lambda e, o=o, kh=kh: e.dma_start(out=outv[:, kh * 8:(kh + 1) * 8, o:o + 512],
                                                             in_=h_own[:, kh * 8:(kh + 1) * 8, o:o + 512]),
                      r=[("h", k, ti) for k in range(kh * 8, kh * 8 + 8)], dma=("out", ti * 2 + kh), cost=3.0)
        P.end_region()

        assert state["next_use"] == len(pieces), (state, len(pieces))

        P.resolve()
        dma_names = sorted(set(op.dma[0] for op in P.ops if op.dma is not None), key=str)
        from contextlib import ExitStack
        with ExitStack() as es:
            sems = {}
            for e_ in ("pe", "act", "dve"):
                sems[("eng", e_)] = es.enter_context(nc.semaphore("s_" + e_))
            for dn in dma_names:
                sems[("dma", dn)] = es.enter_context(nc.semaphore("d_%s_%s" % (dn[0], dn[1])))
            block = es.enter_context(nc.Block())
            by_eng = {}
            for op in P.ops:
                by_eng.setdefault(op.eng, []).append(op)

            def runner(name):
                def f(e):
                    for op in by_eng.get(name, []):
                        for (k, v) in op.waits:
                            e.wait_ge(sems[k], v)
                        if op.fn is None:
                            continue
                        ins = op.fn(e)
                        if op.dma is not None:
                            ins.then_inc(sems[("dma", op.dma[0])], 16)
                        elif op.milestone:
                            ins.then_inc(sems[("eng", name)], 1)
                    if name == "sp":
                        for dn, cntv in P.dma_counts.items():
                            if dn[0] in ("out", "dbg"):
                                e.wait_ge(sems[("dma", dn)], cntv)
                return f
            block.sync(runner("sp"))
            block.gpsimd(runner("pool"))
            block.tensor(runner("pe"))
            block.scalar(runner("act"))
            block.vector(runner("dve"))
    return nc


def _tables(rpb, j):
    rpb = np.asarray(rpb, np.float32)
    out = np.full((5, 8, 128, 768), -1e30, np.float32)
    qi = np.arange(128)
    qr, qc = qi // 64, qi % 64
    cs = np.clip(qc - 8, 0, 48)
    for ti, qb in enumerate((0, 1, 2, 6, 7)):
        lo, nkb, nk = _geom(qb)
        idx = np.arange(nk)
        e = lo * 2 + idx // 64
        kc = idx % 64
        kr = 16 * j - 4 + e
        r = 16 * j + 2 * qb + qr
        rs = np.clip(r - 4, 0, 56)
        valid = ((kr[None, :] >= rs[:, None]) & (kr[None, :] < rs[:, None] + 8)
                 & (kr[None, :] >= 0) & (kr[None, :] < 64)
                 & (kc[None, :] >= cs[:, None]) & (kc[None, :] < cs[:, None] + 16))
        rr = np.clip(kr[None, :] - r[:, None] + 7, 0, 14)
        rc = np.clip(kc[None, :] - qc[:, None] + 15, 0, 30)
        for hh in range(8):
            vals = rpb[hh][rr, rc]
            out[ti, hh, :, :nk] = np.where(valid, vals, np.float32(-1e30))
    return out


def _pm(v):
    v = np.asarray(v, np.float32).reshape(-1, 128)
    return np.ascontiguousarray(v.T)


_NC_CACHE = {}


def _prepare(inputs):
    x = np.asarray(inputs["x"], np.float32)
    p = np.asarray(inputs["p"], np.float32)[0]
    gains = np.concatenate([
        _pm(inputs["ffn1_norm"][0]), _pm(inputs["mix_norm"][0]), _pm(inputs["ffn2_norm"][0]),
        _pm(inputs["ple_norm"][0]), _pm(inputs["final_norm"]),
        _pm(inputs["attn_out_norm"][0]), _pm(inputs["conv_out_norm"][0]),
        _pm(inputs["conv_w"][0][0]), _pm(inputs["conv_w"][0][1]), _pm(inputs["conv_w"][0][2]),
        _pm(inputs["conv_b"][0])], axis=1)
    assert gains.shape == (128, 128)
    gains = np.ascontiguousarray(gains, np.float32)
    ident = np.eye(128, dtype=np.float32)
    shared = {nm: np.ascontiguousarray(np.asarray(inputs[nm], np.float32)[0]) for nm in
              ("ffn1_wg", "ffn1_wu", "ffn1_wd", "w_in", "w_out", "ffn2_wg", "ffn2_wu", "ffn2_wd",
               "ple_w_gate", "ple_w_proj")}
    tabs = [_tables(inputs["rpb"][0], j) for j in range(4)]
    in_maps = []
    for c in range(8):
        b, j = divmod(c, 4)
        t0 = 1024 * j
        xb = x[b]
        xo = np.ascontiguousarray(xb[t0:t0 + 1024].T)
        xh = np.zeros((HALO, D), np.float32)
        lo = t0 - 256
        if lo >= 0:
            xh[0:256] = xb[lo:t0]
        hi = t0 + 1024
        if hi + 192 <= 4096:
            xh[256:448] = xb[hi:hi + 192]
        xh = np.ascontiguousarray(xh.T)
        pTc = np.ascontiguousarray(p[b, t0:t0 + 1024].T)
        m = {"xo": xo, "xh": xh, "pT": pTc, "gains": gains, "ident": ident, "tbl": tabs[j]}
        m.update(shared)
        in_maps.append(m)
    return in_maps


def kernel(**inputs):
    in_maps = _prepare(inputs)
    if "nc" not in _NC_CACHE:
        _NC_CACHE["nc"] = build_program()
    nc = _NC_CACHE["nc"]
    res = run_bass_kernel_spmd(nc, in_maps, core_ids=list(range(8)))
    out = np.empty((2, 4096, D), np.float32)
    for c in range(8):
        b, j = divmod(c, 4)
        out[b, 1024 * j:1024 * (j + 1)] = res.results[c]["outT"].T
    return out
```

```python
import numpy as np
import concourse.bass as bass
import concourse.mybir as mybir
from concourse.bass_utils import run_bass_kernel_spmd

F32 = mybir.dt.float32
BF16 = mybir.dt.bfloat16
AF = mybir.ActivationFunctionType
ALU = mybir.AluOpType
AX = mybir.AxisListType

D = 2048
NK = 16
DFF = 5632
NF = 44
OWN = 1024
HALO = 448
EPS = 1e-6
NB = 8
SAME_ENG_SYNC = True

G_FFN1, G_MIX, G_FFN2, G_PLE, G_FIN = 0, 16, 32, 48, 64
G_ATT, G_CONV, G_CW0, G_CW1, G_CW2, G_CB = 80, 88, 96, 104, 112, 120

T0 = ("own", 0, 512)
T1 = ("own", 512, 512)
T2 = ("halo", 0, 448)


class Op:
    __slots__ = ("eng", "fn", "deps", "dma", "milestone", "seq", "waits")

    def __init__(self, eng, fn, deps, dma):
        self.eng, self.fn, self.deps, self.dma = eng, fn, deps, dma
        self.milestone = False
        self.seq = 0
        self.waits = []


class Prog:
    COMPUTE = ("pe", "act", "dve")

    def __init__(self):
        self.ops = []
        self.last_w = {}
        self.readers = {}
        self.dma_counts = {}
        self.last_on = {}
        self.rec = None

    def begin_region(self):
        assert self.rec is None
        self.rec = []

    def end_region(self, schedule=True, pess=1.3, dma_serial=False):
        items = self.rec
        self.rec = None
        n = len(items)
        order = list(range(n))
        if schedule and n > 1:
            last_w, readers = {}, {}
            deps = [set() for _ in range(n)]
            for i, (eng, fn, r, w, dma, cost, prio) in enumerate(items):
                for k in r:
                    j = last_w.get(k)
                    if j is not None:
                        deps[i].add(j)
                for k in w:
                    j = last_w.get(k)
                    if j is not None:
                        deps[i].add(j)
                    for j in readers.get(k, ()):
                        deps[i].add(j)
                ws = set(w)
                for k in w:
                    last_w[k] = i
                    readers[k] = []
                for k in r:
                    if k not in ws:
                        readers.setdefault(k, []).append(i)
                deps[i].discard(i)
            succ = [[] for _ in range(n)]
            indeg = [0] * n
            for i in range(n):
                indeg[i] = len(deps[i])
                for j in deps[i]:
                    succ[j].append(i)
            ready_t = [0.0] * n
            finish = [0.0] * n
            eng_free = {}
            ready = [i for i in range(n) if indeg[i] == 0]
            order = []
            while ready:
                best = None
                for i in ready:
                    eng, fn, r, w, dma, cost, prio = items[i]
                    st = max(eng_free.get(eng, 0.0), ready_t[i])
                    key = (st, prio, i)
                    if best is None or key < best[0]:
                        best = (key, i)
                (st, _, _), i = best
                ready.remove(i)
                eng, fn, r, w, dma, cost, prio = items[i]
                c = cost if cost is not None else 0.3
                if eng != "pe" and dma is None:
                    c = c * pess
                if dma is not None:
                    eng_free[eng] = st + (c if dma_serial else 0.3 * c)
                    finish[i] = st + c + (1.0 if dma_serial else 2.0)
                else:
                    eng_free[eng] = st + c
                    finish[i] = st + c
                order.append(i)
                for j in succ[i]:
                    lat = (0.1 if (items[j][0] == eng and dma is None) else 0.2) * (pess if eng != "pe" else 1.0)
                    ready_t[j] = max(ready_t[j], finish[i] + lat)
                    indeg[j] -= 1
                    if indeg[j] == 0:
                        ready.append(j)
            assert len(order) == n
            self.last_makespan = max(finish) if n else 0.0
        for i in order:
            eng, fn, r, w, dma, cost, prio = items[i]
            self.add(eng, fn, r=r, w=w, dma=dma)

    def add(self, eng, fn, r=(), w=(), dma=None, extra=(), cost=None, prio=1):
        if self.rec is not None:
            assert not extra
            self.rec.append((eng, fn, tuple(r), tuple(w), dma, cost, prio))
            return None
        i = len(self.ops)
        deps = set(extra)
        for k in r:
            j = self.last_w.get(k)
            if j is not None:
                deps.add(j)
        for k in w:
            j = self.last_w.get(k)
            if j is not None:
                deps.add(j)
            for j in self.readers.get(k, ()):
                deps.add(j)
        dm = None
        if dma is not None:
            self.dma_counts[dma] = self.dma_counts.get(dma, 0) + 16
            dm = (dma, self.dma_counts[dma])
        op = Op(eng, fn, deps, dm)
        self.ops.append(op)
        wset = set(w)
        for k in w:
            self.last_w[k] = i
            self.readers[k] = []
        for k in r:
            if k in wset:
                continue
            lst = self.readers.setdefault(k, [])
            if dm is None:
                lst[:] = [j for j in lst if not (self.ops[j].dma is None and self.ops[j].eng == eng)]
            lst.append(i)
        if fn is not None:
            self.last_on[eng] = i
        return i

    def barrier(self):
        lasts = [self.last_on[e] for e in self.COMPUTE if e in self.last_on]
        for e in ("pe", "act", "dve", "pool", "sp"):
            self.add(e, None, extra=lasts)

    def _skip(self, prod, cons):
        if prod.dma is not None:
            return False
        if prod.eng != cons.eng:
            return False
        if prod.eng == "pe":
            return True
        return not SAME_ENG_SYNC

    def resolve(self):
        ops = self.ops
        for op in ops:
            for j in op.deps:
                pj = ops[j]
                if pj.dma is None and not self._skip(pj, op):
                    pj.milestone = True
        cnt = {}
        for op in ops:
            if op.dma is None and op.milestone:
                assert op.fn is not None
                cnt[op.eng] = cnt.get(op.eng, 0) + 1
                op.seq = cnt[op.eng]
        seen = {}
        for op in ops:
            waits = {}
            for j in op.deps:
                pj = ops[j]
                if pj.dma is not None:
                    key, val = ("dma", pj.dma[0]), pj.dma[1]
                else:
                    if self._skip(pj, op):
                        continue
                    key, val = ("eng", pj.eng), pj.seq
                if waits.get(key, 0) < val:
                    waits[key] = val
            sn = seen.setdefault(op.eng, {})
            op.waits = []
            for k, v in waits.items():
                if sn.get(k, 0) < v:
                    op.waits.append((k, v))
                    sn[k] = v


def _geom(qb):
    if qb == 0:
        lo, n = 0, 6
    elif qb == 7:
        lo, n = 6, 6
    else:
        lo, n = qb, 5
    nk = sum(64 if b == 11 else 128 for b in range(lo, lo + n))
    return lo, n, nk


def _tbl_index(qb):
    return {0: 0, 1: 1, 6: 3, 7: 4}.get(qb, 2)


def build_program(debug=False):
    nc = bass.Bass("TRN2", target_bir_lowering=False)

    def dram_in(name, shape):
        return nc.dram_tensor(name, list(shape), F32, kind="ExternalInput").ap()

    xo = dram_in("xo", [D, OWN])
    xh = dram_in("xh", [D, HALO])
    pT = dram_in("pT", [256, OWN])
    gains_d = dram_in("gains", [128, 128])
    ident_d = dram_in("ident", [128, 128])
    tbl_d = dram_in("tbl", [5, 8, 128, 768])
    w = {}
    for nm, shp in (("ffn1_wg", [D, DFF]), ("ffn1_wu", [D, DFF]), ("ffn1_wd", [DFF, D]),
                    ("w_in", [D, 6144]), ("w_out", [D, D]),
                    ("ffn2_wg", [D, DFF]), ("ffn2_wu", [D, DFF]), ("ffn2_wd", [DFF, D]),
                    ("ple_w_gate", [D, D]), ("ple_w_proj", [256, D])):
        w[nm] = dram_in(nm, shp)
    outT = nc.dram_tensor("outT", [D, OWN], F32, kind="ExternalOutput").ap()
    dbg = {}
    if debug:
        for nm, shp in (("dbg_h1", [D, OWN]), ("dbg_mixed", [D, OWN]), ("dbg_h2", [D, OWN]),
                        ("dbg_h3", [D, OWN])):
            dbg[nm] = nc.dram_tensor(nm, shp, F32, kind="ExternalOutput").ap()

    P = Prog()
    ARENA_F32 = 53200
    with (
        nc.sbuf_tensor("arena", [128, ARENA_F32], F32) as ar,
        nc.psum_tensor("ps", [128, 4096], F32) as ps,
    ):
        def cv(off, nbytes, dtype):
            assert off % 4 == 0 and nbytes % 4 == 0
            assert off + nbytes <= ARENA_F32 * 4, (off, nbytes)
            v = ar[:, off // 4:(off + nbytes) // 4]
            return v.bitcast(BF16) if dtype is BF16 else v

        off = 0

        def take(nbytes):
            nonlocal off
            o = off
            off += nbytes
            return o

        h_own = cv(take(65536), 65536, F32).rearrange("p (k t) -> p k t", k=NK)
        a_own = cv(take(32768), 32768, BF16).rearrange("p (k t) -> p k t", k=NK)
        a_halo = cv(take(14336), 14336, BF16).rearrange("p (k t) -> p k t", k=NK)
        gains = cv(take(512), 512, F32)
        ident = cv(take(256), 256, BF16)
        ones = cv(take(256), 256, BF16)
        identf = cv(take(512), 512, F32)
        ring = [cv(take(4096), 4096, BF16) for _ in range(NB)]
        LOC = off
        LOC_SIZE = ARENA_F32 * 4 - LOC

        def loc(o, nbytes, dtype):
            assert o + nbytes <= LOC_SIZE, (o, nbytes, LOC_SIZE)
            return cv(LOC + o, nbytes, dtype)

        h_halo = loc(0, 28672, F32).rearrange("p (k t) -> p k t", k=NK)
        NT = 28672
        sq = [loc(NT + i * 1024, 1024, BF16) for i in range(2)]
        lnt = loc(NT + 2048, 2048, F32)
        rstd = [loc(NT + 4096 + i * 2048, 2048, F32) for i in range(2)]
        HID1 = NT + 8192
        sq2 = [loc(i * 1024, 1024, BF16) for i in range(2)]
        lnt2 = loc(2048, 2048, F32)
        rstd2 = [loc(4096 + i * 2048, 2048, F32) for i in range(2)]
        HID2 = 8192

        def bank(b, n=512, c0=0):
            return ps[:, b * 512 + c0: b * 512 + c0 + n]

        def h_ap(k, t):
            reg, s, n = t
            return (h_own if reg == "own" else h_halo)[:, k, s:s + n]

        def a_ap(k, t):
            reg, s, n = t
            return (a_own if reg == "own" else a_halo)[:, k, s:s + n]

        def tkey(t):
            return 2 if t[0] == "halo" else t[1] // 512

        pieces = []
        state = {"next_dma": 0, "next_use": 0}

        def plan_col(W, c0):
            pieces.append(("col", W.rearrange("(k p) f -> p k f", p=128)[:, :, c0:c0 + 128]))

        def plan_row(W, r0):
            pieces.append(("row", W[r0:r0 + 128, :]))

        def emit_dma(i, extra_r=()):
            kind, src = pieces[i]
            slot = i % NB
            dst = ring[slot].rearrange("p (k f) -> p k f", k=NK) if kind == "col" else ring[slot]
            P.add("pool", lambda e, dst=dst, src=src: e.dma_start(out=dst, in_=src),
                  r=list(extra_r), w=[("ring", slot)], dma=("ring", slot), cost=5.5)

        def use_piece():
            i = state["next_use"]
            state["next_use"] += 1
            assert i < state["next_dma"], "weight piece used before its DMA was emitted"
            kind, _ = pieces[i]
            slot = i % NB
            view = ring[slot].rearrange("p (k f) -> p k f", k=NK) if kind == "col" else ring[slot]
            return i, view, ("ring", slot)

        def release(i):
            j = state["next_dma"]
            if j < len(pieces):
                assert j == i + NB, (i, j)
                emit_dma(j)
                state["next_dma"] += 1

        def plan_ffn(wg, wu, wd, GC):
            ng = NF // GC
            for g in range(ng + 1):
                if g < ng:
                    for gi in range(GC):
                        fc = g * GC + gi
                        plan_col(wg, fc * 128)
                        plan_col(wu, fc * 128)
                if g >= 1:
                    for gi in range(GC):
                        fc = (g - 1) * GC + gi
                        plan_row(wd, fc * 128)

        GC1, GC2 = 4, 4
        plan_ffn(w["ffn1_wg"], w["ffn1_wu"], w["ffn1_wd"], GC1)
        for hh in range(8):
            for base in (0, 1024, 2048):
                plan_col(w["w_in"], base + hh * 128)
        for c in range(8):
            for base in (5120, 4096, 3072):
                plan_col(w["w_in"], base + c * 128)
        for dc in range(NK):
            plan_col(w["w_out"], dc * 128)
        plan_ffn(w["ffn2_wg"], w["ffn2_wu"], w["ffn2_wd"], GC2)
        for dc in range(NK):
            plan_col(w["ple_w_gate"], dc * 128)

        P.begin_region()
        P.add("sp", lambda e: e.dma_start(out=gains, in_=gains_d), w=[("gains",)], dma=("misc", 0), cost=0.5)
        P.add("sp", lambda e: e.dma_start(out=identf, in_=ident_d), w=[("identf",)], dma=("misc", 1), cost=0.5)
        xo_v = xo.rearrange("(k p) t -> p k t", p=128)
        xh_v = xh.rearrange("(k p) t -> p k t", p=128)
        P.add("sp", lambda e: e.dma_start(out=h_own[:, 0:8, 0:512], in_=xo_v[:, 0:8, 0:512]),
              w=[("h", k, 0) for k in range(8)], dma=("x", 0), cost=6.0)
        P.add("sp", lambda e: e.dma_start(out=h_own[:, 8:16, 0:512], in_=xo_v[:, 8:16, 0:512]),
              w=[("h", k, 0) for k in range(8, NK)], dma=("x", 3), cost=6.0)
        P.add("sp", lambda e: e.dma_start(out=h_own[:, :, 512:1024], in_=xo_v[:, :, 512:1024]),
              w=[("h", k, 1) for k in range(NK)], dma=("x", 1), cost=14.0)
        P.add("sp", lambda e: e.dma_start(out=h_halo[:, :, :], in_=xh_v),
              w=[("h", k, 2) for k in range(NK)], dma=("x", 2), cost=12.0)
        for i in range(NB):
            emit_dma(i, extra_r=([("h", 15, 0)] if i >= 2 else []) + ([("h", 15, 1)] if i >= 4 else []))
        state["next_dma"] = NB
        P.add("dve", lambda e: e.memset(ones, 1.0), w=[("ones",)], cost=0.2)
        P.add("dve", lambda e: e.tensor_copy(out=ident, in_=identf), r=[("identf",)], w=[("ident",)], cost=0.3)

        def norm(tiles, gcol, sqb, lntb, rstdb, out_fn=None, out_keys=None):
            for ti, t in enumerate(tiles):
                n = t[2]
                tk = tkey(t)
                for k in range(NK):
                    s_ = sqb[k % 2][:, :n]
                    P.add("act", lambda e, s_=s_, k=k, t=t: e.activation(out=s_, in_=h_ap(k, t), func=AF.Square),
                          r=[("h", k, tk)], w=[("sq", k % 2)], cost=n / 1000.0 + 0.1, prio=0)
                    P.add("pe", lambda e, s_=s_, k=k, n=n: e.matmul(bank(7, n), lhsT=ones, rhs=s_,
                                                                   start=(k == 0), stop=(k == NK - 1)),
                          r=[("sq", k % 2), ("ones",)], w=[("ps", 7)], cost=n / 2400.0 + 0.02, prio=0)
                rs = rstdb[ti % 2][:, :n]
                P.add("act", lambda e, n=n: e.activation(out=lntb[:, :n], in_=bank(7, n), func=AF.Ln,
                                                         scale=1.0 / D, bias=EPS),
                      r=[("ps", 7)], w=[("lnt",)], cost=n / 1000.0 + 0.2, prio=0)
                P.add("act", lambda e, n=n, rs=rs: e.activation(out=rs, in_=lntb[:, :n], func=AF.Exp, scale=-0.5),
                      r=[("lnt",)], w=[("rstd", ti % 2)], cost=n / 1000.0 + 0.2, prio=0)
                for k in range(NK):
                    if out_fn is None:
                        o_ap, okeys = a_ap(k, t), [("a", k, tk)]
                    else:
                        o_ap, okeys = out_fn(k, t), out_keys(k, t)
                    P.add("dve", lambda e, o_ap=o_ap, k=k, t=t, rs=rs: e.scalar_tensor_tensor(
                        out=o_ap, in0=h_ap(k, t), scalar=gains[:, gcol + k:gcol + k + 1], in1=rs,
                        op0=ALU.mult, op1=ALU.mult),
                        r=[("h", k, tk), ("rstd", ti % 2), ("gains",)], w=okeys, cost=n / 960.0 + 0.07, prio=0)

        def ffn(tiles, GC, hid_off, pre=None, post=None, region_open=False, serial_dma=False):
            ng = NF // GC
            ntok = sum(t[2] for t in tiles)
            toff = []
            o = 0
            for t in tiles:
                toff.append(o)
                o += t[2]
            hid = [loc(hid_off + s * GC * ntok * 2, GC * ntok * 2, BF16).rearrange("p (g t) -> p g t", g=GC)
                   for s in range(2)]
            silu_off = hid_off + 2 * GC * ntok * 2
            silu = [loc(silu_off + i * 2048, 2048, F32) for i in range(2)]
            cnt = {"gu": 0, "dn": 0}

            def gate_up(g):
                for gi in range(GC):
                    ig, vg, kg = use_piece()
                    iu, vu, ku = use_piece()
                    for ti, t in enumerate(tiles):
                        n = t[2]
                        tk = tkey(t)
                        c = cnt["gu"]
                        cnt["gu"] += 1
                        bA, bB = c % 2, 2 + c % 2

                        def mm(e, wv, b, t=t, n=n):
                            for k in range(NK):
                                ins = e.matmul(bank(b, n), lhsT=wv[:, k, :], rhs=a_ap(k, t),
                                               start=(k == 0), stop=(k == NK - 1))
                            return ins
                        P.add("pe", lambda e, vg=vg, bA=bA, mm=mm: mm(e, vg, bA),
                              r=[kg] + [("a", k, tk) for k in range(NK)], w=[("ps", bA)], cost=16 * n / 2400.0 + 0.05)
                        P.add("pe", lambda e, vu=vu, bB=bB, mm=mm: mm(e, vu, bB),
                              r=[ku] + [("a", k, tk) for k in range(NK)], w=[("ps", bB)], cost=16 * n / 2400.0 + 0.05)
                        sl = silu[c % 2][:, :n]
                        P.add("act", lambda e, sl=sl, bA=bA, n=n: e.activation(out=sl, in_=bank(bA, n), func=AF.Silu),
                              r=[("ps", bA)], w=[("silu", c % 2)], cost=n / 1000.0 + 0.1)
                        hv = hid[g % 2][:, gi, toff[ti]:toff[ti] + n]
                        P.add("dve", lambda e, hv=hv, sl=sl, bB=bB, n=n: e.tensor_tensor(
                            out=hv, in0=bank(bB, n), in1=sl, op=ALU.mult),
                            r=[("ps", bB), ("silu", c % 2)], w=[("hid", g % 2, gi, ti)], cost=n / 960.0 + 0.07)
                    release(ig)
                    release(iu)

            def down(g):
                pcs = [use_piece() for _ in range(GC)]
                for ti, t in enumerate(tiles):
                    n = t[2]
                    tk = tkey(t)
                    for dc in range(NK):
                        c = cnt["dn"]
                        cnt["dn"] += 1
                        b = 4 + c % 3

                        def mm(e, b=b, n=n, dc=dc, ti=ti):
                            for gi in range(GC):
                                ins = e.matmul(bank(b, n), lhsT=pcs[gi][1][:, dc * 128:(dc + 1) * 128],
                                               rhs=hid[g % 2][:, gi, toff[ti]:toff[ti] + n],
                                               start=(gi == 0), stop=(gi == GC - 1))
                            return ins
                        P.add("pe", mm, r=[p_[2] for p_ in pcs] + [("hid", g % 2, gi, ti) for gi in range(GC)],
                              w=[("ps", b)], cost=GC * n / 2400.0 + 0.03)
                        P.add("dve", lambda e, b=b, n=n, dc=dc, t=t: e.scalar_tensor_tensor(
                            out=h_ap(dc, t), in0=bank(b, n), scalar=0.5, in1=h_ap(dc, t),
                            op0=ALU.mult, op1=ALU.add),
                            r=[("ps", b), ("h", dc, tk)], w=[("h", dc, tk)], cost=n / 960.0 + 0.07)
                for p_ in pcs:
                    release(p_[0])

            for g in range(ng + 1):
                if g == 0:
                    if not region_open:
                        P.begin_region()
                    if pre is not None:
                        pre()
                    gate_up(0)
                    P.end_region(dma_serial=serial_dma)
                elif g == ng:
                    P.begin_region()
                    down(g - 1)
                    if post is not None:
                        post()
                    P.end_region()
                else:
                    gate_up(g)
                    down(g - 1)

        tiles_ext = [T0, T1, T2]
        tiles_own = [T0, T1]

        mixed = loc(0, 32768, BF16).rearrange("p (k t) -> p k t", k=NK)
        AT = 32768
        QO = AT + 16512
        qT = [loc(QO + s * 2048, 2048, BF16) for s in range(2)]
        kT = [loc(QO + 4096 + s * 2944, 2944, BF16) for s in range(2)]
        vv = [loc(QO + 9984 + s * 3168, 3168, BF16).rearrange("p (b d) -> p b d", b=12) for s in range(2)]
        assert QO >= HID1 + 2 * GC1 * 1472 and QO + 16320 <= LOC_SIZE
        s3 = [loc(AT + i * 3072, 3072, F32) for i in range(3)]
        p_bf = [loc(AT + 9216 + i * 1536, 1536, BF16) for i in range(2)]
        pT_sb = [loc(AT + 12288 + i * 1536, 1536, BF16) for i in range(2)]
        yb = [loc(AT + 15360 + i * 256, 256, BF16) for i in range(2)]
        junk = [loc(AT + 15872 + i * 256, 256, BF16) for i in range(2)]
        stt = [loc(AT + 16384 + i * 64, 64, F32) for i in range(2)]
        assert AT + 16512 <= LOC_SIZE
        SB = (2, 5)
        MB = (4, 7)

        def psT(par):
            return ps[:, MB[par] * 512: MB[par] * 512 + 384].bitcast(BF16)

        def psY(par):
            return ps[:, MB[par] * 512 + 384: MB[par] * 512 + 448].bitcast(BF16)

        def v_ones():
            for s in range(2):
                P.add("dve", lambda e, s=s: e.memset(vv[s][:, :, 128:129], 1.0), r=[("a", 15, 2)],
                      w=[("v", s, j) for j in range(3)], cost=0.2)

        att_scale = 128.0 ** -0.5
        pcnt = {"p": 0, "nb": 2}

        def pbank():
            c = pcnt["p"]
            pcnt["p"] += 1
            return c % pcnt["nb"]

        def blk_cols(b):
            if b < 2:
                return ("halo", b * 128, 128)
            if b < 10:
                return ("own", (b - 2) * 128, 128)
            if b == 10:
                return ("halo", 256, 128)
            return ("halo", 384, 64)

        def emit_tbl(gi):
            hh, qb = divmod(gi, 8)
            lo, nkb, nk = _geom(qb)
            src = tbl_d[_tbl_index(qb), hh, :, 0:nk]
            dst = s3[gi % 3][:, 0:nk]
            P.add("sp", lambda e, dst=dst, src=src: e.dma_start(out=dst, in_=src),
                  w=[("s3", gi % 3)], dma=("tbl", gi % 3), cost=1.5, prio=0)

        def proj_mm(wv, wkey, b, n, rhs_fn, tks, out_ap=None):
            for q4 in range(4):
                def mm(e, q4=q4):
                    for k in range(q4 * 4, q4 * 4 + 4):
                        ins = e.matmul(out_ap if out_ap is not None else bank(b, n), lhsT=wv[:, k, :], rhs=rhs_fn(k),
                                       start=(k == 0), stop=(k == NK - 1))
                    return ins
                P.add("pe", mm, r=[wkey] + [("a", k, tk) for k in range(q4 * 4, q4 * 4 + 4) for tk in tks],
                      w=[("ps", b)], cost=4 * max(n, 128) / 2400.0 + 0.02, prio=1)

        def proj(hh):
            s = hh % 2
            iq, vq, kq = use_piece()
            for t in tiles_own:
                b = pbank()
                proj_mm(vq, kq, b, 512, lambda k, t=t: a_ap(k, t), [tkey(t)])
                P.add("act", lambda e, b=b, t=t, s=s: e.mul(qT[s][:, t[1]:t[1] + 512], bank(b), att_scale),
                      r=[("ps", b)], w=[("q", s, t[1] // 512)], cost=0.6, prio=1)
            release(iq)
            ik, vk, kk_ = use_piece()
            for t in tiles_ext:
                b = pbank()
                n = t[2]
                proj_mm(vk, kk_, b, n, lambda k, t=t: a_ap(k, t), [tkey(t)])
                if t[0] == "own":
                    d0 = 256 + t[1]
                    P.add("dve", lambda e, b=b, d0=d0, s=s: e.tensor_copy(out=kT[s][:, d0:d0 + 512], in_=bank(b)),
                          r=[("ps", b)], w=[("k", s, 1 + t[1] // 512)], cost=0.6, prio=1)
                else:
                    P.add("dve", lambda e, b=b, s=s: e.tensor_copy(out=kT[s][:, 0:256], in_=bank(b, 256)),
                          r=[("ps", b)], w=[("k", s, 0)], cost=0.33, prio=1)
                    P.add("dve", lambda e, b=b, s=s: e.tensor_copy(out=kT[s][:, 1280:1472], in_=bank(b, 192, 256)),
                          r=[("ps", b)], w=[("k", s, 3)], cost=0.26, prio=1)
            release(ik)
            iv, vvw, kv = use_piece()
            for j in range(3):
                b = pbank()
                for bb in range(4):
                    blk = j * 4 + bb
                    reg, c0, ntok = blk_cols(blk)
                    asrc = a_own if reg == "own" else a_halo
                    tk = 2 if reg == "halo" else c0 // 512

                    def mm(e, asrc=asrc, c0=c0, ntok=ntok, bb=bb, vvw=vvw, b=b):
                        for k in range(NK):
                            ins = e.matmul(ps[0:ntok, b * 512 + bb * 128: b * 512 + (bb + 1) * 128],
                                           lhsT=asrc[:, k, c0:c0 + ntok], rhs=vvw[:, k, :],
                                           start=(k == 0), stop=(k == NK - 1))
                        return ins
                    P.add("pe", mm, r=[kv] + [("a", k, tk) for k in range(NK)], w=[("ps", b)], cost=0.95, prio=1)
                if j < 2:
                    P.add("act", lambda e, j=j, s=s, b=b: e.copy(
                        vv[s][:, j * 4:(j + 1) * 4, 0:128],
                        bank(b).rearrange("p (b d) -> p b d", b=4)),
                        r=[("ps", b)], w=[("v", s, j)], cost=0.6, prio=1)
                else:
                    def cpv(e, s=s, b=b):
                        e.copy(vv[s][:, 8:11, 0:128], bank(b, 384).rearrange("p (b d) -> p b d", b=3))
                        return e.copy(vv[s][0:64, 11, 0:128], ps[0:64, b * 512 + 384: b * 512 + 512])
                    P.add("act", cpv, r=[("ps", b)], w=[("v", s, j)], cost=0.8, prio=1)
            release(iv)

        def kkeys(s, lo, nkb):
            c0, c1 = lo * 128, min((lo + nkb) * 128, 1472)
            out = []
            for i, (a0, a1) in enumerate(((0, 256), (256, 768), (768, 1280), (1280, 1472))):
                if c0 < a1 and c1 > a0:
                    out.append(("k", s, i))
            return out

        def attn_block(gi):
            hh, qb = divmod(gi, 8)
            s = hh % 2
            par = gi % 2
            lo, nkb, nk = _geom(qb)
            k0 = lo * 128
            n1 = min(nk, 512)
            n2 = nk - n1
            sb = SB[par]
            mb = MB[par]
            sc = s3[gi % 3]
            st = stt[par]
            emit_tbl(gi)

            def mm_s(e):
                ins = e.matmul(ps[:, sb * 512: sb * 512 + n1], lhsT=qT[s][:, qb * 128:(qb + 1) * 128],
                               rhs=kT[s][:, k0:k0 + n1], start=True, stop=True)
                if n2 > 0:
                    ins = e.matmul(ps[:, (sb + 1) * 512: (sb + 1) * 512 + n2], lhsT=qT[s][:, qb * 128:(qb + 1) * 128],
                                   rhs=kT[s][:, k0 + n1:k0 + nk], start=True, stop=True)
                return ins
            P.add("pe", mm_s, r=[("q", s, qb // 4)] + kkeys(s, lo, nkb), w=[("ps", sb), ("ps", sb + 1)],
                  cost=0.32, prio=0)
            P.add("dve", lambda e: e.tensor_tensor(out=sc[:, :nk], in0=ps[:, sb * 512: sb * 512 + nk],
                                                   in1=sc[:, :nk], op=ALU.add),
                  r=[("ps", sb), ("ps", sb + 1), ("s3", gi % 3)], w=[("s3", gi % 3)], cost=nk / 960.0 + 0.1, prio=0)
            P.add("dve", lambda e: e.tensor_reduce(out=st[:, 0:1], in_=sc[:, :nk], axis=AX.X,
                                                   op=ALU.max, negate=True),
                  r=[("s3", gi % 3)], w=[("st", par, 0)], cost=nk / 960.0 + 0.1, prio=0)
            P.add("act", lambda e: e.activation(out=p_bf[par][:, :nk], in_=sc[:, :nk], func=AF.Exp,
                                                bias=st[:, 0:1], scale=1.0),
                  r=[("s3", gi % 3), ("st", par, 0)], w=[("p", par)], cost=nk / 1000.0 + 0.05, prio=0)

            def mm_t(e):
                for jb in range(nkb):
                    wdt = 64 if lo + jb == 11 else 128
                    ins = e.transpose(out=psT(par)[0:wdt, jb * 128:(jb + 1) * 128],
                                      in_=p_bf[par][:, jb * 128: jb * 128 + wdt], identity=ident)
                return ins
            P.add("pe", mm_t, r=[("p", par), ("ident",)], w=[("ps", mb)], cost=nkb * 0.058 + 0.15, prio=0)
            if lo + nkb - 1 == 11:
                nf = (nkb - 1) * 128

                def cp(e):
                    e.copy(pT_sb[par][:, :nf], psT(par)[:, :nf])
                    return e.copy(pT_sb[par][0:64, nf:nf + 128], psT(par)[0:64, nf:nf + 128])
                P.add("act", cp, r=[("ps", mb)], w=[("pT", par)], cost=nkb * 0.1 + 0.3, prio=0)
            else:
                P.add("act", lambda e: e.copy(pT_sb[par][:, :nkb * 128], psT(par)[:, :nkb * 128]),
                      r=[("ps", mb)], w=[("pT", par)], cost=nkb * 0.1 + 0.08, prio=0)

            def mm_o(e):
                for jb in range(nkb):
                    kk = 64 if lo + jb == 11 else 128
                    ins = e.matmul(bank(mb, 129), lhsT=pT_sb[par][0:kk, jb * 128:(jb + 1) * 128],
                                   rhs=vv[s][0:kk, lo + jb, 0:129], start=(jb == 0), stop=(jb == nkb - 1))
                return ins
            vks = sorted(set(("v", s, (lo + jb) // 4) for jb in range(nkb)))
            P.add("pe", mm_o, r=[("pT", par)] + vks, w=[("ps", mb)], cost=nkb * 0.058 + 0.15, prio=0)
            P.add("dve", lambda e: e.reciprocal(out=st[:, 1:2], in_=ps[:, mb * 512 + 128: mb * 512 + 129]),
                  r=[("ps", mb)], w=[("st", par, 1)], cost=0.08, prio=0)
            P.add("act", lambda e: e.activation(out=junk[par], in_=bank(mb, 128), func=AF.Square,
                                                scale=st[:, 1:2], accum_out=st[:, 2:3]),
                  r=[("ps", mb), ("st", par, 1)], w=[("st", par, 2), ("junk", par)], cost=0.36, prio=0)
            P.add("act", lambda e: e.activation(out=st[:, 3:4], in_=st[:, 2:3], func=AF.Ln,
                                                scale=1.0 / 128, bias=EPS),
                  r=[("st", par, 2)], w=[("st", par, 3)], cost=0.2, prio=0)
            P.add("act", lambda e: e.activation(out=st[:, 4:5], in_=st[:, 3:4], func=AF.Exp, scale=-0.5),
                  r=[("st", par, 3)], w=[("st", par, 4)], cost=0.2, prio=0)
            P.add("dve", lambda e: e.tensor_tensor(out=st[:, 5:6], in0=st[:, 4:5], in1=st[:, 1:2], op=ALU.mult),
                  r=[("st", par, 4), ("st", par, 1)], w=[("st", par, 5)], cost=0.16, prio=0)
            P.add("dve", lambda e: e.tensor_scalar(out=yb[par], in0=bank(mb, 128), scalar1=st[:, 5:6], scalar2=None,
                                                   op0=ALU.mult),
                  r=[("ps", mb), ("st", par, 5)], w=[("yb", par)], cost=0.35, prio=0)
            P.add("pe", lambda e: e.transpose(out=psY(par), in_=yb[par], identity=ident),
                  r=[("yb", par), ("ident",)], w=[("ps", mb)], cost=0.2, prio=0)
            P.add("act", lambda e: e.mul(mixed[:, hh, qb * 128:(qb + 1) * 128], psY(par),
                                         gains[:, G_ATT + hh:G_ATT + hh + 1]),
                  r=[("ps", mb), ("gains",)], w=[("mixed", hh, qb // 4)], cost=0.3, prio=0)

        def ffn1_post():
            norm(tiles_ext, G_MIX, sq, lnt, rstd)
            v_ones()
            proj(0)

        ffn(tiles_ext, GC1, HID1, pre=lambda: norm(tiles_ext, G_FFN1, sq, lnt, rstd),
            post=ffn1_post, region_open=True, serial_dma=True)
        if debug:
            P.add("sp", lambda e: e.dma_start(out=dbg["dbg_h1"].rearrange("(k p) t -> p k t", p=128), in_=h_own[:, :, :]),
                  r=[("h", k, tt) for k in range(NK) for tt in (0, 1)], dma=("dbg", 0))
        P.barrier()

        P.begin_region()
        proj(1)
        for hh in range(8):
            for qb in range(8):
                attn_block(hh * 8 + qb)
            if hh + 2 < 8:
                proj(hh + 2)
        P.end_region(pess=1.5)
        print("[sched] heads region est makespan us", getattr(P, "last_makespan", None))

        P.barrier()
        CO = 32768
        u_sb = [loc(CO + i * 4104, 4104, F32) for i in range(2)]
        m_sb = [loc(CO + 8208 + i * 4104, 4104, F32) for i in range(2)]
        y_sb = [loc(CO + 16416 + i * 2048, 2048, F32) for i in range(2)]
        yc_sb = [loc(CO + 20512 + i * 2048, 2048, F32) for i in range(2)]
        sqc = loc(CO + 24608, 1024, BF16)
        lntc = loc(CO + 25632, 2048, F32)
        rstdc = loc(CO + 27680, 2048, F32)
        assert CO + 29728 <= LOC_SIZE
        segs = [("own", 0, 512, 1), ("own", 512, 512, 513), ("edge", 255, 2, None)]
        pcnt["nb"] = 7
        P.begin_region()
        for c in range(8):
            cp = c % 2
            us, ms = u_sb[cp], m_sb[cp]
            iu, vu, ku = use_piece()
            ic, vc, kc = use_piece()
            ib, vb, kb = use_piece()
            for (reg, c0, n, m0) in segs:
                b = pbank()
                asrc = a_own if reg == "own" else a_halo
                tks = [c0 // 512] if reg == "own" else [2]
                proj_mm(vu, ku, b, n, lambda k, asrc=asrc, c0=c0, n=n: asrc[:, k, c0:c0 + n], tks)
                if reg == "own":
                    P.add("act", lambda e, b=b, m0=m0, us=us: e.copy(us[:, m0:m0 + 512], bank(b)),
                          r=[("ps", b)], w=[("u_sb", cp, m0)], cost=0.6)
                else:
                    P.add("act", lambda e, b=b, us=us: e.copy(us[:, 0:1], bank(b, 1, 0)),
                          r=[("ps", b)], w=[("u_sb", cp, 0)], cost=0.2)
                    P.add("act", lambda e, b=b, us=us: e.copy(us[:, 1025:1026], bank(b, 1, 1)),
                          r=[("ps", b)], w=[("u_sb", cp, 1025)], cost=0.2)
            release(iu)
            for (reg, c0, n, m0) in segs:
                b = pbank()
                asrc = a_own if reg == "own" else a_halo
                tks = [c0 // 512] if reg == "own" else [2]
                proj_mm(vc, kc, b, n, lambda k, asrc=asrc, c0=c0, n=n: asrc[:, k, c0:c0 + n], tks)
                if reg == "own":
                    P.add("dve", lambda e, b=b, m0=m0, us=us, ms=ms: e.tensor_tensor(
                        out=ms[:, m0:m0 + 512], in0=bank(b), in1=us[:, m0:m0 + 512], op=ALU.mult),
                        r=[("ps", b), ("u_sb", cp, m0)], w=[("m", cp, m0)], cost=0.63)
                else:
                    P.add("dve", lambda e, b=b, us=us, ms=ms: e.tensor_tensor(
                        out=ms[:, 0:1], in0=bank(b, 1, 0), in1=us[:, 0:1], op=ALU.mult),
                        r=[("ps", b), ("u_sb", cp, 0)], w=[("m", cp, 0)], cost=0.1)
                    P.add("dve", lambda e, b=b, us=us, ms=ms: e.tensor_tensor(
                        out=ms[:, 1025:1026], in0=bank(b, 1, 1), in1=us[:, 1025:1026], op=ALU.mult),
                        r=[("ps", b), ("u_sb", cp, 1025)], w=[("m", cp, 1025)], cost=0.1)
            release(ic)
            mkeys = [("m", cp, 0), ("m", cp, 1), ("m", cp, 513), ("m", cp, 1025)]
            for ti, t in enumerate(tiles_own):
                b = pbank()
                o = t[1]
                ys, ycs = y_sb[ti], yc_sb[ti]
                proj_mm(vb, kb, b, 512, lambda k, t=t: a_ap(k, t), [tkey(t)])
                P.add("dve", lambda e, o=o, c=c, ms=ms, ys=ys: e.tensor_scalar(
                    out=ys, in0=ms[:, o:o + 512], scalar1=gains[:, G_CW0 + c:G_CW0 + c + 1], scalar2=None,
                    op0=ALU.mult), r=mkeys + [("gains",)], w=[("y", ti)], cost=0.6)
                P.add("dve", lambda e, o=o, c=c, ms=ms, ys=ys: e.scalar_tensor_tensor(
                    out=ys, in0=ms[:, o + 1:o + 513], scalar=gains[:, G_CW1 + c:G_CW1 + c + 1], in1=ys,
                    op0=ALU.mult, op1=ALU.add), r=mkeys + [("y", ti)], w=[("y", ti)], cost=0.6)
                P.add("dve", lambda e, o=o, c=c, ms=ms, ys=ys: e.scalar_tensor_tensor(
                    out=ys, in0=ms[:, o + 2:o + 514], scalar=gains[:, G_CW2 + c:G_CW2 + c + 1], in1=ys,
                    op0=ALU.mult, op1=ALU.add), r=mkeys + [("y", ti)], w=[("y", ti)], cost=0.6)
                P.add("dve", lambda e, b=b, c=c, ys=ys, ycs=ycs: e.scalar_tensor_tensor(
                    out=ycs, in0=ys, scalar=gains[:, G_CB + c:G_CB + c + 1], in1=bank(b),
                    op0=ALU.add, op1=ALU.mult), r=[("y", ti), ("ps", b)], w=[("yc", ti)], cost=0.63)
                P.add("act", lambda e, ycs=ycs: e.activation(out=sqc, in_=ycs, func=AF.Square),
                      r=[("yc", ti)], w=[("sqc",)], cost=0.6)
                P.add("pe", lambda e: e.matmul(bank(7), lhsT=ones, rhs=sqc, start=True, stop=True),
                      r=[("sqc",), ("ones",)], w=[("ps", 7)], cost=0.25)
                P.add("act", lambda e: e.activation(out=lntc, in_=bank(7), func=AF.Ln, scale=1.0 / 128, bias=EPS),
                      r=[("ps", 7)], w=[("lntc",)], cost=0.7)
                P.add("act", lambda e: e.activation(out=rstdc, in_=lntc, func=AF.Exp, scale=-0.5),
                      r=[("lntc",)], w=[("rstdc",)], cost=0.7)
                P.add("dve", lambda e, o=o, c=c, ycs=ycs: e.scalar_tensor_tensor(
                    out=mixed[:, 8 + c, o:o + 512], in0=ycs, scalar=gains[:, G_CONV + c:G_CONV + c + 1],
                    in1=rstdc, op0=ALU.mult, op1=ALU.mult),
                    r=[("yc", ti), ("rstdc",), ("gains",)], w=[("mixed", 8 + c, ti)], cost=0.6)
            release(ib)

        if debug:
            P.add("pool", lambda e: e.dma_start(out=dbg["dbg_mixed"].rearrange("(k p) t -> p k t", p=128), in_=mixed[:, :, :]),
                  r=[("mixed", mc, tt) for mc in range(NK) for tt in (0, 1)], dma=("dbg", 1))

        for dc in range(NK):
            io, vo, ko = use_piece()
            for ti, t in enumerate(tiles_own):
                b = pbank()
                o = t[1]
                for q4 in range(4):
                    def mm(e, b=b, o=o, vo=vo, q4=q4):
                        for mc in range(q4 * 4, q4 * 4 + 4):
                            ins = e.matmul(bank(b), lhsT=vo[:, mc, :], rhs=mixed[:, mc, o:o + 512],
                                           start=(mc == 0), stop=(mc == NK - 1))
                        return ins
                    P.add("pe", mm, r=[ko] + [("mixed", mc, ti) for mc in range(q4 * 4, q4 * 4 + 4)], w=[("ps", b)],
                          cost=0.88)
                P.add("dve", lambda e, b=b, dc=dc, t=t: e.tensor_tensor(out=h_ap(dc, t), in0=bank(b), in1=h_ap(dc, t),
                                                                        op=ALU.add),
                      r=[("ps", b), ("h", dc, ti)], w=[("h", dc, ti)], cost=0.63)
            release(io)
        sq3 = [loc(CO + i * 1024, 1024, BF16) for i in range(2)]
        lnt3 = loc(CO + 2048, 2048, F32)
        rstd3 = [loc(CO + 4096 + i * 2048, 2048, F32) for i in range(2)]
        norm(tiles_own, G_FFN2, sq3, lnt3, rstd3)

        HID2N = 40960
        pt_bf = loc(8192, 4096, BF16).rearrange("p (k t) -> p k t", k=2)
        wple = loc(12288, 8192, BF16).rearrange("p (k f) -> p k f", k=2)
        sig = [loc(20480 + i * 2048, 2048, F32) for i in range(2)]
        P.add("pool", lambda e: e.dma_start(out=pt_bf, in_=pT.rearrange("(k p) t -> p k t", p=128)),
              r=[("a", 15, 0), ("a", 15, 1)], w=[("pt",)], dma=("misc", 2), cost=3.0)
        P.add("pool", lambda e: e.dma_start(out=wple, in_=w["ple_w_proj"].rearrange("(k p) f -> p k f", p=128)),
              r=[("a", 15, 0), ("a", 15, 1)], w=[("wple",)], dma=("misc", 3), cost=4.0)
        ffn(tiles_own, GC2, HID2N, pre=None,
            post=lambda: norm(tiles_own, G_PLE, sq2, lnt2, rstd2), region_open=True)
        print("[sched] conv+wout+ffn2-head region est makespan us", getattr(P, "last_makespan", None))
        if debug:
            P.add("sp", lambda e: e.dma_start(out=dbg["dbg_h2"].rearrange("(k p) t -> p k t", p=128), in_=h_own[:, :, :]),
                  r=[("h", k, tt) for k in range(NK) for tt in (0, 1)], dma=("dbg", 2))
        if debug:
            P.add("sp", lambda e: e.dma_start(out=dbg["dbg_h3"].rearrange("(k p) t -> p k t", p=128), in_=h_own[:, :, :]),
                  r=[("h", k, tt) for k in range(NK) for tt in (0, 1)], dma=("dbg", 3))

        P.begin_region()
        cc = 0
        for dc in range(NK):
            ig, vg, kg = use_piece()
            for ti, t in enumerate(tiles_own):
                o = t[1]
                bA, bB = cc % 2, 2 + cc % 2
                si = cc % 2
                cc += 1
                for q4 in range(4):
                    def mm(e, bA=bA, t=t, vg=vg, q4=q4):
                        for k in range(q4 * 4, q4 * 4 + 4):
                            ins = e.matmul(bank(bA), lhsT=vg[:, k, :], rhs=a_ap(k, t), start=(k == 0), stop=(k == NK - 1))
                        return ins
                    P.add("pe", mm, r=[kg] + [("a", k, ti) for k in range(q4 * 4, q4 * 4 + 4)], w=[("ps", bA)], cost=0.88)

                def mm2(e, bB=bB, o=o, dc=dc):
                    for k2 in range(2):
                        ins = e.matmul(bank(bB), lhsT=wple[:, k2, dc * 128:(dc + 1) * 128], rhs=pt_bf[:, k2, o:o + 512],
                                       start=(k2 == 0), stop=(k2 == 1))
                    return ins
                P.add("pe", mm2, r=[("wple",), ("pt",)], w=[("ps", bB)], cost=0.45)
                P.add("act", lambda e, bA=bA, si=si: e.activation(out=sig[si], in_=bank(bA), func=AF.Sigmoid),
                      r=[("ps", bA)], w=[("sig", si)], cost=0.62)
                P.add("dve", lambda e, bB=bB, si=si: e.tensor_tensor(out=sig[si], in0=bank(bB), in1=sig[si], op=ALU.mult),
                      r=[("ps", bB), ("sig", si)], w=[("sig", si)], cost=0.62)
                P.add("dve", lambda e, si=si, dc=dc, t=t: e.tensor_tensor(out=h_ap(dc, t), in0=sig[si], in1=h_ap(dc, t),
                                                                         op=ALU.add),
                      r=[("sig", si), ("h", dc, ti)], w=[("h", dc, ti)], cost=0.62)
            release(ig)
        norm(tiles_own, G_FIN, sq2, lnt2, rstd2, out_fn=lambda k, t: h_ap(k, t),
             out_keys=lambda k, t: [("h", k, tkey(t))])
        outv = outT.rearrange("(k p) t -> p k t", p=128)
        for ti, t in enumerate(tiles_own):
            o = t[1]
            for kh in range(2):
                P.add("sp", lambda e, o=o, kh=kh: e.dma_start(out=outv[:, kh * 8:(kh + 1) * 8, o:o + 512],
                                                             in_=h_own[:, kh * 8:(kh + 1) * 8, o:o + 512]),
                      r=[("h", k, ti) for k in range(kh * 8, kh * 8 + 8)], dma=("out", ti * 2 + kh), cost=3.0)
        P.end_region()

        assert state["next_use"] == len(pieces), (state, len(pieces))

        P.resolve()
        dma_names = sorted(set(op.dma[0] for op in P.ops if op.dma is not None), key=str)
        from contextlib import ExitStack
        with ExitStack() as es:
            sems = {}
            for e_ in ("pe", "act", "dve"):
                sems[("eng", e_)] = es.enter_context(nc.semaphore("s_" + e_))
            for dn in dma_names:
                sems[("dma", dn)] = es.enter_context(nc.semaphore("d_%s_%s" % (dn[0], dn[1])))
            block = es.enter_context(nc.Block())
            by_eng = {}
            for op in P.ops:
                by_eng.setdefault(op.eng, []).append(op)

            def runner(name):
                def f(e):
                    for op in by_eng.get(name, []):
                        for (k, v) in op.waits:
                            e.wait_ge(sems[k], v)
                        if op.fn is None:
                            continue
                        ins = op.fn(e)
                        if op.dma is not None:
                            ins.then_inc(sems[("dma", op.dma[0])], 16)
                        elif op.milestone:
                            ins.then_inc(sems[("eng", name)], 1)
                    if name == "sp":
                        for dn, cntv in P.dma_counts.items():
                            if dn[0] in ("out", "dbg"):
                                e.wait_ge(sems[("dma", dn)], cntv)
                return f
            block.sync(runner("sp"))
            block.gpsimd(runner("pool"))
            block.tensor(runner("pe"))
            block.scalar(runner("act"))
            block.vector(runner("dve"))
    return nc


def _tables(rpb, j):
    rpb = np.asarray(rpb, np.float32)
    out = np.full((5, 8, 128, 768), -1e30, np.float32)
    qi = np.arange(128)
    qr, qc = qi // 64, qi % 64
    cs = np.clip(qc - 8, 0, 48)
    for ti, qb in enumerate((0, 1, 2, 6, 7)):
        lo, nkb, nk = _geom(qb)
        idx = np.arange(nk)
        e = lo * 2 + idx // 64
        kc = idx % 64
        kr = 16 * j - 4 + e
        r = 16 * j + 2 * qb + qr
        rs = np.clip(r - 4, 0, 56)
        valid = ((kr[None, :] >= rs[:, None]) & (kr[None, :] < rs[:, None] + 8)
                 & (kr[None, :] >= 0) & (kr[None, :] < 64)
                 & (kc[None, :] >= cs[:, None]) & (kc[None, :] < cs[:, None] + 16))
        rr = np.clip(kr[None, :] - r[:, None] + 7, 0, 14)
        rc = np.clip(kc[None, :] - qc[:, None] + 15, 0, 30)
        for hh in range(8):
            vals = rpb[hh][rr, rc]
            out[ti, hh, :, :nk] = np.where(valid, vals, np.float32(-1e30))
    return out


def _pm(v):
    v = np.asarray(v, np.float32).reshape(-1, 128)
    return np.ascontiguousarray(v.T)


_NC_CACHE = {}


def _prepare(inputs):
    x = np.asarray(inputs["x"], np.float32)
    p = np.asarray(inputs["p"], np.float32)[0]
    gains = np.concatenate([
        _pm(inputs["ffn1_norm"][0]), _pm(inputs["mix_norm"][0]), _pm(inputs["ffn2_norm"][0]),
        _pm(inputs["ple_norm"][0]), _pm(inputs["final_norm"]),
        _pm(inputs["attn_out_norm"][0]), _pm(inputs["conv_out_norm"][0]),
        _pm(inputs["conv_w"][0][0]), _pm(inputs["conv_w"][0][1]), _pm(inputs["conv_w"][0][2]),
        _pm(inputs["conv_b"][0])], axis=1)
    assert gains.shape == (128, 128)
    gains = np.ascontiguousarray(gains, np.float32)
    ident = np.eye(128, dtype=np.float32)
    shared = {nm: np.ascontiguousarray(np.asarray(inputs[nm], np.float32)[0]) for nm in
              ("ffn1_wg", "ffn1_wu", "ffn1_wd", "w_in", "w_out", "ffn2_wg", "ffn2_wu", "ffn2_wd",
               "ple_w_gate", "ple_w_proj")}
    tabs = [_tables(inputs["rpb"][0], j) for j in range(4)]
    in_maps = []
    for c in range(8):
        b, j = divmod(c, 4)
        t0 = 1024 * j
        xb = x[b]
        xo = np.ascontiguousarray(xb[t0:t0 + 1024].T)
        xh = np.zeros((HALO, D), np.float32)
        lo = t0 - 256
        if lo >= 0:
            xh[0:256] = xb[lo:t0]
        hi = t0 + 1024
        if hi + 192 <= 4096:
            xh[256:448] = xb[hi:hi + 192]
        xh = np.ascontiguousarray(xh.T)
        pTc = np.ascontiguousarray(p[b, t0:t0 + 1024].T)
        m = {"xo": xo, "xh": xh, "pT": pTc, "gains": gains, "ident": ident, "tbl": tabs[j]}
        m.update(shared)
        in_maps.append(m)
    return in_maps


def kernel(**inputs):
    in_maps = _prepare(inputs)
    if "nc" not in _NC_CACHE:
        _NC_CACHE["nc"] = build_program()
    nc = _NC_CACHE["nc"]
    res = run_bass_kernel_spmd(nc, in_maps, core_ids=list(range(8)))
    out = np.empty((2, 4096, D), np.float32)
    for c in range(8):
        b, j = divmod(c, 4)
        out[b, 1024 * j:1024 * (j + 1)] = res.results[c]["outT"].T
    return out
```

```python
import numpy as np
import concourse.bass as bass
import concourse.mybir as mybir
from concourse.bass_utils import run_bass_kernel_spmd

F32 = mybir.dt.float32
BF16 = mybir.dt.bfloat16
AF = mybir.ActivationFunctionType
ALU = mybir.AluOpType
AX = mybir.AxisListType

D = 2048
NK = 16
DFF = 5632
NF = 44
OWN = 1024
HALO = 448
EPS = 1e-6
NB = 8
SAME_ENG_SYNC = True

G_FFN1, G_MIX, G_FFN2, G_PLE, G_FIN = 0, 16, 32, 48, 64
G_ATT, G_CONV, G_CW0, G_CW1, G_CW2, G_CB = 80, 88, 96, 104, 112, 120

T0 = ("own", 0, 512)
T1 = ("own", 512, 512)
T2 = ("halo", 0, 448)


class Op:
    __slots__ = ("eng", "fn", "deps", "dma", "milestone", "seq", "waits")

    def __init__(self, eng, fn, deps, dma):
        self.eng, self.fn, self.deps, self.dma = eng, fn, deps, dma
        self.milestone = False
        self.seq = 0
        self.waits = []


class Prog:
    COMPUTE = ("pe", "act", "dve")

    def __init__(self):
        self.ops = []
        self.last_w = {}
        self.readers = {}
        self.dma_counts = {}
        self.last_on = {}
        self.rec = None

    def begin_region(self):
        assert self.rec is None
        self.rec = []

    def end_region(self, schedule=True, pess=1.3, dma_serial=False):
        items = self.rec
        self.rec = None
        n = len(items)
        order = list(range(n))
        if schedule and n > 1:
            last_w, readers = {}, {}
            deps = [set() for _ in range(n)]
            for i, (eng, fn, r, w, dma, cost, prio) in enumerate(items):
                for k in r:
                    j = last_w.get(k)
                    if j is not None:
                        deps[i].add(j)
                for k in w:
                    j = last_w.get(k)
                    if j is not None:
                        deps[i].add(j)
                    for j in readers.get(k, ()):
                        deps[i].add(j)
                ws = set(w)
                for k in w:
                    last_w[k] = i
                    readers[k] = []
                for k in r:
                    if k not in ws:
                        readers.setdefault(k, []).append(i)
                deps[i].discard(i)
            succ = [[] for _ in range(n)]
            indeg = [0] * n
            for i in range(n):
                indeg[i] = len(deps[i])
                for j in deps[i]:
                    succ[j].append(i)
            ready_t = [0.0] * n
            finish = [0.0] * n
            eng_free = {}
            ready = [i for i in range(n) if indeg[i] == 0]
            order = []
            while ready:
                best = None
                for i in ready:
                    eng, fn, r, w, dma, cost, prio = items[i]
                    st = max(eng_free.get(eng, 0.0), ready_t[i])
                    key = (st, prio, i)
                    if best is None or key < best[0]:
                        best = (key, i)
                (st, _, _), i = best
                ready.remove(i)
                eng, fn, r, w, dma, cost, prio = items[i]
                c = cost if cost is not None else 0.3
                if eng != "pe" and dma is None:
                    c = c * pess
                if dma is not None:
                    eng_free[eng] = st + (c if dma_serial else 0.3 * c)
                    finish[i] = st + c + (1.0 if dma_serial else 2.0)
                else:
                    eng_free[eng] = st + c
                    finish[i] = st + c
                order.append(i)
                for j in succ[i]:
                    lat = (0.1 if (items[j][0] == eng and dma is None) else 0.2) * (pess if eng != "pe" else 1.0)
                    ready_t[j] = max(ready_t[j], finish[i] + lat)
                    indeg[j] -= 1
                    if indeg[j] == 0:
                        ready.append(j)
            assert len(order) == n
            self.last_makespan = max(finish) if n else 0.0
        for i in order:
            eng, fn, r, w, dma, cost, prio = items[i]
            self.add(eng, fn, r=r, w=w, dma=dma)

    def add(self, eng, fn, r=(), w=(), dma=None, extra=(), cost=None, prio=1):
        if self.rec is not None:
            assert not extra
            self.rec.append((eng, fn, tuple(r), tuple(w), dma, cost, prio))
            return None
        i = len(self.ops)
        deps = set(extra)
        for k in r:
            j = self.last_w.get(k)
            if j is not None:
                deps.add(j)
        for k in w:
            j = self.last_w.get(k)
            if j is not None:
                deps.add(j)
            for j in self.readers.get(k, ()):
                deps.add(j)
        dm = None
        if dma is not None:
            self.dma_counts[dma] = self.dma_counts.get(dma, 0) + 16
            dm = (dma, self.dma_counts[dma])
        op = Op(eng, fn, deps, dm)
        self.ops.append(op)
        wset = set(w)
        for k in w:
            self.last_w[k] = i
            self.readers[k] = []
        for k in r:
            if k in wset:
                continue
            lst = self.readers.setdefault(k, [])
            if dm is None:
                lst[:] = [j for j in lst if not (self.ops[j].dma is None and self.ops[j].eng == eng)]
            lst.append(i)
        if fn is not None:
            self.last_on[eng] = i
        return i

    def barrier(self):
        lasts = [self.last_on[e] for e in self.COMPUTE if e in self.last_on]
        for e in ("pe", "act", "dve", "pool", "sp"):
            self.add(e, None, extra=lasts)

    def _skip(self, prod, cons):
        if prod.dma is not None:
            return False
        if prod.eng != cons.eng:
            return False
        if prod.eng == "pe":
            return True
        return not SAME_ENG_SYNC

    def resolve(self):
        ops = self.ops
        for op in ops:
            for j in op.deps:
                pj = ops[j]
                if pj.dma is None and not self._skip(pj, op):
                    pj.milestone = True
        cnt = {}
        for op in ops:
            if op.dma is None and op.milestone:
                assert op.fn is not None
                cnt[op.eng] = cnt.get(op.eng, 0) + 1
                op.seq = cnt[op.eng]
        seen = {}
        for op in ops:
            waits = {}
            for j in op.deps:
                pj = ops[j]
                if pj.dma is not None:
                    key, val = ("dma", pj.dma[0]), pj.dma[1]
                else:
                    if self._skip(pj, op):
                        continue
                    key, val = ("eng", pj.eng), pj.seq
                if waits.get(key, 0) < val:
                    waits[key] = val
            sn = seen.setdefault(op.eng, {})
            op.waits = []
            for k, v in waits.items():
                if sn.get(k, 0) < v:
                    op.waits.append((k, v))
                    sn[k] = v


def _geom(qb):
    if qb == 0:
        lo, n = 0, 6
    elif qb == 7:
        lo, n = 6, 6
    else:
        lo, n = qb, 5
    nk = sum(64 if b == 11 else 128 for b in range(lo, lo + n))
    return lo, n, nk


def _tbl_index(qb):
    return {0: 0, 1: 1, 6: 3, 7: 4}.get(qb, 2)


def build_program(debug=False):
    nc = bass.Bass("TRN2", target_bir_lowering=False)

    def dram_in(name, shape):
        return nc.dram_tensor(name, list(shape), F32, kind="ExternalInput").ap()

    xo = dram_in("xo", [D, OWN])
    xh = dram_in("xh", [D, HALO])
    pT = dram_in("pT", [256, OWN])
    gains_d = dram_in("gains", [128, 128])
    ident_d = dram_in("ident", [128, 128])
    tbl_d = dram_in("tbl", [5, 8, 128, 768])
    w = {}
    for nm, shp in (("ffn1_wg", [D, DFF]), ("ffn1_wu", [D, DFF]), ("ffn1_wd", [DFF, D]),
                    ("w_in", [D, 6144]), ("w_out", [D, D]),
                    ("ffn2_wg", [D, DFF]), ("ffn2_wu", [D, DFF]), ("ffn2_wd", [DFF, D]),
                    ("ple_w_gate", [D, D]), ("ple_w_proj", [256, D])):
        w[nm] = dram_in(nm, shp)
    outT = nc.dram_tensor("outT", [D, OWN], F32, kind="ExternalOutput").ap()
    dbg = {}
    if debug:
        for nm, shp in (("dbg_h1", [D, OWN]), ("dbg_mixed", [D, OWN]), ("dbg_h2", [D, OWN]),
                        ("dbg_h3", [D, OWN])):
            dbg[nm] = nc.dram_tensor(nm, shp, F32, kind="ExternalOutput").ap()

    P = Prog()
    ARENA_F32 = 53200
    with (
        nc.sbuf_tensor("arena", [128, ARENA_F32], F32) as ar,
        nc.psum_tensor("ps", [128, 4096], F32) as ps,
    ):
        def cv(off, nbytes, dtype):
            assert off % 4 == 0 and nbytes % 4 == 0
            assert off + nbytes <= ARENA_F32 * 4, (off, nbytes)
            v = ar[:, off // 4:(off + nbytes) // 4]
            return v.bitcast(BF16) if dtype is BF16 else v

        off = 0

        def take(nbytes):
            nonlocal off
            o = off
            off += nbytes
            return o

        h_own = cv(take(65536), 65536, F32).rearrange("p (k t) -> p k t", k=NK)
        a_own = cv(take(32768), 32768, BF16).rearrange("p (k t) -> p k t", k=NK)
        a_halo = cv(take(14336), 14336, BF16).rearrange("p (k t) -> p k t", k=NK)
        gains = cv(take(512), 512, F32)
        ident = cv(take(256), 256, BF16)
        ones = cv(take(256), 256, BF16)
        identf = cv(take(512), 512, F32)
        ring = [cv(take(4096), 4096, BF16) for _ in range(NB)]
        LOC = off
        LOC_SIZE = ARENA_F32 * 4 - LOC

        def loc(o, nbytes, dtype):
            assert o + nbytes <= LOC_SIZE, (o, nbytes, LOC_SIZE)
            return cv(LOC + o, nbytes, dtype)

        h_halo = loc(0, 28672, F32).rearrange("p (k t) -> p k t", k=NK)
        NT = 28672
        sq = [loc(NT + i * 1024, 1024, BF16) for i in range(2)]
        lnt = loc(NT + 2048, 2048, F32)
        rstd = [loc(NT + 4096 + i * 2048, 2048, F32) for i in range(2)]
        HID1 = NT + 8192
        sq2 = [loc(i * 1024, 1024, BF16) for i in range(2)]
        lnt2 = loc(2048, 2048, F32)
        rstd2 = [loc(4096 + i * 2048, 2048, F32) for i in range(2)]
        HID2 = 8192

        def bank(b, n=512, c0=0):
            return ps[:, b * 512 + c0: b * 512 + c0 + n]

        def h_ap(k, t):
            reg, s, n = t
            return (h_own if reg == "own" else h_halo)[:, k, s:s + n]

        def a_ap(k, t):
            reg, s, n = t
            return (a_own if reg == "own" else a_halo)[:, k, s:s + n]

        def tkey(t):
            return 2 if t[0] == "halo" else t[1] // 512

        pieces = []
        state = {"next_dma": 0, "next_use": 0}

        def plan_col(W, c0):
            pieces.append(("col", W.rearrange("(k p) f -> p k f", p=128)[:, :, c0:c0 + 128]))

        def plan_row(W, r0):
            pieces.append(("row", W[r0:r0 + 128, :]))

        def emit_dma(i, extra_r=()):
            kind, src = pieces[i]
            slot = i % NB
            dst = ring[slot].rearrange("p (k f) -> p k f", k=NK) if kind == "col" else ring[slot]
            P.add("pool", lambda e, dst=dst, src=src: e.dma_start(out=dst, in_=src),
                  r=list(extra_r), w=[("ring", slot)], dma=("ring", slot), cost=5.5)

        def use_piece():
            i = state["next_use"]
            state["next_use"] += 1
            assert i < state["next_dma"], "weight piece used before its DMA was emitted"
            kind, _ = pieces[i]
            slot = i % NB
            view = ring[slot].rearrange("p (k f) -> p k f", k=NK) if kind == "col" else ring[slot]
            return i, view, ("ring", slot)

        def release(i):
            j = state["next_dma"]
            if j < len(pieces):
                assert j == i + NB, (i, j)
                emit_dma(j)
                state["next_dma"] += 1

        def plan_ffn(wg, wu, wd, GC):
            ng = NF // GC
            for g in range(ng + 1):
                if g < ng:
                    for gi in range(GC):
                        fc = g * GC + gi
                        plan_col(wg, fc * 128)
                        plan_col(wu, fc * 128)
                if g >= 1:
                    for gi in range(GC):
                        fc = (g - 1) * GC + gi
                        plan_row(wd, fc * 128)

        GC1, GC2 = 4, 4
        plan_ffn(w["ffn1_wg"], w["ffn1_wu"], w["ffn1_wd"], GC1)
        for hh in range(8):
            for base in (0, 1024, 2048):
                plan_col(w["w_in"], base + hh * 128)
        for c in range(8):
            for base in (5120, 4096, 3072):
                plan_col(w["w_in"], base + c * 128)
        for dc in range(NK):
            plan_col(w["w_out"], dc * 128)
        plan_ffn(w["ffn2_wg"], w["ffn2_wu"], w["ffn2_wd"], GC2)
        for dc in range(NK):
            plan_col(w["ple_w_gate"], dc * 128)

        P.begin_region()
        P.add("sp", lambda e: e.dma_start(out=gains, in_=gains_d), w=[("gains",)], dma=("misc", 0), cost=0.5)
        P.add("sp", lambda e: e.dma_start(out=identf, in_=ident_d), w=[("identf",)], dma=("misc", 1), cost=0.5)
        xo_v = xo.rearrange("(k p) t -> p k t", p=128)
        xh_v = xh.rearrange("(k p) t -> p k t", p=128)
        P.add("sp", lambda e: e.dma_start(out=h_own[:, 0:8, 0:512], in_=xo_v[:, 0:8, 0:512]),
              w=[("h", k, 0) for k in range(8)], dma=("x", 0), cost=6.0)
        P.add("sp", lambda e: e.dma_start(out=h_own[:, 8:16, 0:512], in_=xo_v[:, 8:16, 0:512]),
              w=[("h", k, 0) for k in range(8, NK)], dma=("x", 3), cost=6.0)
        P.add("sp", lambda e: e.dma_start(out=h_own[:, :, 512:1024], in_=xo_v[:, :, 512:1024]),
              w=[("h", k, 1) for k in range(NK)], dma=("x", 1), cost=14.0)
        P.add("sp", lambda e: e.dma_start(out=h_halo[:, :, :], in_=xh_v),
              w=[("h", k, 2) for k in range(NK)], dma=("x", 2), cost=12.0)
        for i in range(NB):
            emit_dma(i, extra_r=([("h", 15, 0)] if i >= 2 else []) + ([("h", 15, 1)] if i >= 4 else []))
        state["next_dma"] = NB
        P.add("dve", lambda e: e.memset(ones, 1.0), w=[("ones",)], cost=0.2)
        P.add("dve", lambda e: e.tensor_copy(out=ident, in_=identf), r=[("identf",)], w=[("ident",)], cost=0.3)

        def norm(tiles, gcol, sqb, lntb, rstdb, out_fn=None, out_keys=None):
            for ti, t in enumerate(tiles):
                n = t[2]
                tk = tkey(t)
                for k in range(NK):
                    s_ = sqb[k % 2][:, :n]
                    P.add("act", lambda e, s_=s_, k=k, t=t: e.activation(out=s_, in_=h_ap(k, t), func=AF.Square),
                          r=[("h", k, tk)], w=[("sq", k % 2)], cost=n / 1000.0 + 0.1, prio=0)
                    P.add("pe", lambda e, s_=s_, k=k, n=n: e.matmul(bank(7, n), lhsT=ones, rhs=s_,
                                                                   start=(k == 0), stop=(k == NK - 1)),
                          r=[("sq", k % 2), ("ones",)], w=[("ps", 7)], cost=n / 2400.0 + 0.02, prio=0)
                rs = rstdb[ti % 2][:, :n]
                P.add("act", lambda e, n=n: e.activation(out=lntb[:, :n], in_=bank(7, n), func=AF.Ln,
                                                         scale=1.0 / D, bias=EPS),
                      r=[("ps", 7)], w=[("lnt",)], cost=n / 1000.0 + 0.2, prio=0)
                P.add("act", lambda e, n=n, rs=rs: e.activation(out=rs, in_=lntb[:, :n], func=AF.Exp, scale=-0.5),
                      r=[("lnt",)], w=[("rstd", ti % 2)], cost=n / 1000.0 + 0.2, prio=0)
                for k in range(NK):
                    if out_fn is None:
                        o_ap, okeys = a_ap(k, t), [("a", k, tk)]
                    else:
                        o_ap, okeys = out_fn(k, t), out_keys(k, t)
                    P.add("dve", lambda e, o_ap=o_ap, k=k, t=t, rs=rs: e.scalar_tensor_tensor(
                        out=o_ap, in0=h_ap(k, t), scalar=gains[:, gcol + k:gcol + k + 1], in1=rs,
                        op0=ALU.mult, op1=ALU.mult),
                        r=[("h", k, tk), ("rstd", ti % 2), ("gains",)], w=okeys, cost=n / 960.0 + 0.07, prio=0)

        def ffn(tiles, GC, hid_off, pre=None, post=None, region_open=False, serial_dma=False):
            ng = NF // GC
            ntok = sum(t[2] for t in tiles)
            toff = []
            o = 0
            for t in tiles:
                toff.append(o)
                o += t[2]
            hid = [loc(hid_off + s * GC * ntok * 2, GC * ntok * 2, BF16).rearrange("p (g t) -> p g t", g=GC)
                   for s in range(2)]
            silu_off = hid_off + 2 * GC * ntok * 2
            silu = [loc(silu_off + i * 2048, 2048, F32) for i in range(2)]
            cnt = {"gu": 0, "dn": 0}

            def gate_up(g):
                for gi in range(GC):
                    ig, vg, kg = use_piece()
                    iu, vu, ku = use_piece()
                    for ti, t in enumerate(tiles):
                        n = t[2]
                        tk = tkey(t)
                        c = cnt["gu"]
                        cnt["gu"] += 1
                        bA, bB = c % 2, 2 + c % 2

                        def mm(e, wv, b, t=t, n=n):
                            for k in range(NK):
                                ins = e.matmul(bank(b, n), lhsT=wv[:, k, :], rhs=a_ap(k, t),
                                               start=(k == 0), stop=(k == NK - 1))
                            return ins
                        P.add("pe", lambda e, vg=vg, bA=bA, mm=mm: mm(e, vg, bA),
                              r=[kg] + [("a", k, tk) for k in range(NK)], w=[("ps", bA)], cost=16 * n / 2400.0 + 0.05)
                        P.add("pe", lambda e, vu=vu, bB=bB, mm=mm: mm(e, vu, bB),
                              r=[ku] + [("a", k, tk) for k in range(NK)], w=[("ps", bB)], cost=16 * n / 2400.0 + 0.05)
                        sl = silu[c % 2][:, :n]
                        P.add("act", lambda e, sl=sl, bA=bA, n=n: e.activation(out=sl, in_=bank(bA, n), func=AF.Silu),
                              r=[("ps", bA)], w=[("silu", c % 2)], cost=n / 1000.0 + 0.1)
                        hv = hid[g % 2][:, gi, toff[ti]:toff[ti] + n]
                        P.add("dve", lambda e, hv=hv, sl=sl, bB=bB, n=n: e.tensor_tensor(
                            out=hv, in0=bank(bB, n), in1=sl, op=ALU.mult),
                            r=[("ps", bB), ("silu", c % 2)], w=[("hid", g % 2, gi, ti)], cost=n / 960.0 + 0.07)
                    release(ig)
                    release(iu)

            def down(g):
                pcs = [use_piece() for _ in range(GC)]
                for ti, t in enumerate(tiles):
                    n = t[2]
                    tk = tkey(t)
                    for dc in range(NK):
                        c = cnt["dn"]
                        cnt["dn"] += 1
                        b = 4 + c % 3

                        def mm(e, b=b, n=n, dc=dc, ti=ti):
                            for gi in range(GC):
                                ins = e.matmul(bank(b, n), lhsT=pcs[gi][1][:, dc * 128:(dc + 1) * 128],
                                               rhs=hid[g % 2][:, gi, toff[ti]:toff[ti] + n],
                                               start=(gi == 0), stop=(gi == GC - 1))
                            return ins
                        P.add("pe", mm, r=[p_[2] for p_ in pcs] + [("hid", g % 2, gi, ti) for gi in range(GC)],
                              w=[("ps", b)], cost=GC * n / 2400.0 + 0.03)
                        P.add("dve", lambda e, b=b, n=n, dc=dc, t=t: e.scalar_tensor_tensor(
                            out=h_ap(dc, t), in0=bank(b, n), scalar=0.5, in1=h_ap(dc, t),
                            op0=ALU.mult, op1=ALU.add),
                            r=[("ps", b), ("h", dc, tk)], w=[("h", dc, tk)], cost=n / 960.0 + 0.07)
                for p_ in pcs:
                    release(p_[0])

            for g in range(ng + 1):
                if g == 0:
                    if not region_open:
                        P.begin_region()
                    if pre is not None:
                        pre()
                    gate_up(0)
                    P.end_region(dma_serial=serial_dma)
                elif g == ng:
                    P.begin_region()
                    down(g - 1)
                    if post is not None:
                        post()
                    P.end_region()
                else:
                    gate_up(g)
                    down(g - 1)

        tiles_ext = [T0, T1, T2]
        tiles_own = [T0, T1]

        mixed = loc(0, 32768, BF16).rearrange("p (k t) -> p k t", k=NK)
        AT = 32768
        QO = AT + 16512
        qT = [loc(QO + s * 2048, 2048, BF16) for s in range(2)]
        kT = [loc(QO + 4096 + s * 2944, 2944, BF16) for s in range(2)]
        vv = [loc(QO + 9984 + s * 3168, 3168, BF16).rearrange("p (b d) -> p b d", b=12) for s in range(2)]
        assert QO >= HID1 + 2 * GC1 * 1472 and QO + 16320 <= LOC_SIZE
        s3 = [loc(AT + i * 3072, 3072, F32) for i in range(3)]
        p_bf = [loc(AT + 9216 + i * 1536, 1536, BF16) for i in range(2)]
        pT_sb = [loc(AT + 12288 + i * 1536, 1536, BF16) for i in range(2)]
        yb = [loc(AT + 15360 + i * 256, 256, BF16) for i in range(2)]
        junk = [loc(AT + 15872 + i * 256, 256, BF16) for i in range(2)]
        stt = [loc(AT + 16384 + i * 64, 64, F32) for i in range(2)]
        assert AT + 16512 <= LOC_SIZE
        SB = (2, 5)
        MB = (4, 7)

        def psT(par):
            return ps[:, MB[par] * 512: MB[par] * 512 + 384].bitcast(BF16)

        def psY(par):
            return ps[:, MB[par] * 512 + 384: MB[par] * 512 + 448].bitcast(BF16)

        def v_ones():
            for s in range(2):
                P.add("dve", lambda e, s=s: e.memset(vv[s][:, :, 128:129], 1.0), r=[("a", 15, 2)],
                      w=[("v", s, j) for j in range(3)], cost=0.2)

        att_scale = 128.0 ** -0.5
        pcnt = {"p": 0, "nb": 2}

        def pbank():
            c = pcnt["p"]
            pcnt["p"] += 1
            return c % pcnt["nb"]

        def blk_cols(b):
            if b < 2:
                return ("halo", b * 128, 128)
            if b < 10:
                return ("own", (b - 2) * 128, 128)
            if b == 10:
                return ("halo", 256, 128)
            return ("halo", 384, 64)

        def emit_tbl(gi):
            hh, qb = divmod(gi, 8)
            lo, nkb, nk = _geom(qb)
            src = tbl_d[_tbl_index(qb), hh, :, 0:nk]
            dst = s3[gi % 3][:, 0:nk]
            alias_r = [("a", k, t_) for k in range(NK) for t_ in range(3)] if gi < 3 else []
            P.add("sp", lambda e, dst=dst, src=src: e.dma_start(out=dst, in_=src),
                  r=alias_r, w=[("s3", gi % 3)], dma=("tbl", gi % 3), cost=1.5, prio=0)

        def proj_mm(wv, wkey, b, n, rhs_fn, tks, out_ap=None):
            for q4 in range(4):
                def mm(e, q4=q4):
                    for k in range(q4 * 4, q4 * 4 + 4):
                        ins = e.matmul(out_ap if out_ap is not None else bank(b, n), lhsT=wv[:, k, :], rhs=rhs_fn(k),
                                       start=(k == 0), stop=(k == NK - 1))
                    return ins
                P.add("pe", mm, r=[wkey] + [("a", k, tk) for k in range(q4 * 4, q4 * 4 + 4) for tk in tks],
                      w=[("ps", b)], cost=4 * max(n, 128) / 2400.0 + 0.02, prio=1)

        def proj(hh):
            s = hh % 2
            iq, vq, kq = use_piece()
            for t in tiles_own:
                b = pbank()
                proj_mm(vq, kq, b, 512, lambda k, t=t: a_ap(k, t), [tkey(t)])
                P.add("act", lambda e, b=b, t=t, s=s: e.mul(qT[s][:, t[1]:t[1] + 512], bank(b), att_scale),
                      r=[("ps", b)], w=[("q", s, t[1] // 512)], cost=0.6, prio=1)
            release(iq)
            ik, vk, kk_ = use_piece()
            for t in tiles_ext:
                b = pbank()
                n = t[2]
                proj_mm(vk, kk_, b, n, lambda k, t=t: a_ap(k, t), [tkey(t)])
                if t[0] == "own":
                    d0 = 256 + t[1]
                    P.add("dve", lambda e, b=b, d0=d0, s=s: e.tensor_copy(out=kT[s][:, d0:d0 + 512], in_=bank(b)),
                          r=[("ps", b)], w=[("k", s, 1 + t[1] // 512)], cost=0.6, prio=1)
                else:
                    P.add("dve", lambda e, b=b, s=s: e.tensor_copy(out=kT[s][:, 0:256], in_=bank(b, 256)),
                          r=[("ps", b)], w=[("k", s, 0)], cost=0.33, prio=1)
                    P.add("dve", lambda e, b=b, s=s: e.tensor_copy(out=kT[s][:, 1280:1472], in_=bank(b, 192, 256)),
                          r=[("ps", b)], w=[("k", s, 3)], cost=0.26, prio=1)
            release(ik)
            iv, vvw, kv = use_piece()
            for j in range(3):
                b = pbank()
                for bb in range(4):
                    blk = j * 4 + bb
                    reg, c0, ntok = blk_cols(blk)
                    asrc = a_own if reg == "own" else a_halo
                    tk = 2 if reg == "halo" else c0 // 512

                    def mm(e, asrc=asrc, c0=c0, ntok=ntok, bb=bb, vvw=vvw, b=b):
                        for k in range(NK):
                            ins = e.matmul(ps[0:ntok, b * 512 + bb * 128: b * 512 + (bb + 1) * 128],
                                           lhsT=asrc[:, k, c0:c0 + ntok], rhs=vvw[:, k, :],
                                           start=(k == 0), stop=(k == NK - 1))
                        return ins
                    P.add("pe", mm, r=[kv] + [("a", k, tk) for k in range(NK)], w=[("ps", b)], cost=0.95, prio=1)
                if j < 2:
                    P.add("act", lambda e, j=j, s=s, b=b: e.copy(
                        vv[s][:, j * 4:(j + 1) * 4, 0:128],
                        bank(b).rearrange("p (b d) -> p b d", b=4)),
                        r=[("ps", b)], w=[("v", s, j)], cost=0.6, prio=1)
                else:
                    def cpv(e, s=s, b=b):
                        e.copy(vv[s][:, 8:11, 0:128], bank(b, 384).rearrange("p (b d) -> p b d", b=3))
                        return e.copy(vv[s][0:64, 11, 0:128], ps[0:64, b * 512 + 384: b * 512 + 512])
                    P.add("act", cpv, r=[("ps", b)], w=[("v", s, j)], cost=0.8, prio=1)
            release(iv)

        def kkeys(s, lo, nkb):
            c0, c1 = lo * 128, min((lo + nkb) * 128, 1472)
            out = []
            for i, (a0, a1) in enumerate(((0, 256), (256, 768), (768, 1280), (1280, 1472))):
                if c0 < a1 and c1 > a0:
                    out.append(("k", s, i))
            return out

        def attn_block(gi):
            hh, qb = divmod(gi, 8)
            s = hh % 2
            par = gi % 2
            lo, nkb, nk = _geom(qb)
            k0 = lo * 128
            n1 = min(nk, 512)
            n2 = nk - n1
            sb = SB[par]
            mb = MB[par]
            sc = s3[gi % 3]
            st = stt[par]
            emit_tbl(gi)

            def mm_s(e):
                ins = e.matmul(ps[:, sb * 512: sb * 512 + n1], lhsT=qT[s][:, qb * 128:(qb + 1) * 128],
                               rhs=kT[s][:, k0:k0 + n1], start=True, stop=True)
                if n2 > 0:
                    ins = e.matmul(ps[:, (sb + 1) * 512: (sb + 1) * 512 + n2], lhsT=qT[s][:, qb * 128:(qb + 1) * 128],
                                   rhs=kT[s][:, k0 + n1:k0 + nk], start=True, stop=True)
                return ins
            P.add("pe", mm_s, r=[("q", s, qb // 4)] + kkeys(s, lo, nkb), w=[("ps", sb), ("ps", sb + 1)],
                  cost=0.32, prio=0)
            P.add("dve", lambda e: e.tensor_tensor(out=sc[:, :nk], in0=ps[:, sb * 512: sb * 512 + nk],
                                                   in1=sc[:, :nk], op=ALU.add),
                  r=[("ps", sb), ("ps", sb + 1), ("s3", gi % 3)], w=[("s3", gi % 3)], cost=nk / 960.0 + 0.1, prio=0)
            P.add("dve", lambda e: e.tensor_reduce(out=st[:, 0:1], in_=sc[:, :nk], axis=AX.X,
                                                   op=ALU.max, negate=True),
                  r=[("s3", gi % 3)], w=[("st", par, 0)], cost=nk / 960.0 + 0.1, prio=0)
            P.add("act", lambda e: e.activation(out=p_bf[par][:, :nk], in_=sc[:, :nk], func=AF.Exp,
                                                bias=st[:, 0:1], scale=1.0),
                  r=[("s3", gi % 3), ("st", par, 0)], w=[("p", par)], cost=nk / 1000.0 + 0.05, prio=0)

            def mm_t(e):
                for jb in range(nkb):
                    wdt = 64 if lo + jb == 11 else 128
                    ins = e.transpose(out=psT(par)[0:wdt, jb * 128:(jb + 1) * 128],
                                      in_=p_bf[par][:, jb * 128: jb * 128 + wdt], identity=ident)
                return ins
            P.add("pe", mm_t, r=[("p", par), ("ident",)], w=[("ps", mb)], cost=nkb * 0.058 + 0.15, prio=0)
            if lo + nkb - 1 == 11:
                nf = (nkb - 1) * 128

                def cp(e):
                    e.copy(pT_sb[par][:, :nf], psT(par)[:, :nf])
                    return e.copy(pT_sb[par][0:64, nf:nf + 128], psT(par)[0:64, nf:nf + 128])
                P.add("act", cp, r=[("ps", mb)], w=[("pT", par)], cost=nkb * 0.1 + 0.3, prio=0)
            else:
                P.add("act", lambda e: e.copy(pT_sb[par][:, :nkb * 128], psT(par)[:, :nkb * 128]),
                      r=[("ps", mb)], w=[("pT", par)], cost=nkb * 0.1 + 0.08, prio=0)

            def mm_o(e):
                for jb in range(nkb):
                    kk = 64 if lo + jb == 11 else 128
                    ins = e.matmul(bank(mb, 129), lhsT=pT_sb[par][0:kk, jb * 128:(jb + 1) * 128],
                                   rhs=vv[s][0:kk, lo + jb, 0:129], start=(jb == 0), stop=(jb == nkb - 1))
                return ins
            vks = sorted(set(("v", s, (lo + jb) // 4) for jb in range(nkb)))
            P.add("pe", mm_o, r=[("pT", par)] + vks, w=[("ps", mb)], cost=nkb * 0.058 + 0.15, prio=0)
            P.add("dve", lambda e: e.reciprocal(out=st[:, 1:2], in_=ps[:, mb * 512 + 128: mb * 512 + 129]),
                  r=[("ps", mb)], w=[("st", par, 1)], cost=0.08, prio=0)
            P.add("act", lambda e: e.activation(out=junk[par], in_=bank(mb, 128), func=AF.Square,
                                                scale=st[:, 1:2], accum_out=st[:, 2:3]),
                  r=[("ps", mb), ("st", par, 1)], w=[("st", par, 2), ("junk", par)], cost=0.36, prio=0)
            P.add("act", lambda e: e.activation(out=st[:, 3:4], in_=st[:, 2:3], func=AF.Ln,
                                                scale=1.0 / 128, bias=EPS),
                  r=[("st", par, 2)], w=[("st", par, 3)], cost=0.2, prio=0)
            P.add("act", lambda e: e.activation(out=st[:, 4:5], in_=st[:, 3:4], func=AF.Exp, scale=-0.5),
                  r=[("st", par, 3)], w=[("st", par, 4)], cost=0.2, prio=0)
            P.add("dve", lambda e: e.tensor_tensor(out=st[:, 5:6], in0=st[:, 4:5], in1=st[:, 1:2], op=ALU.mult),
                  r=[("st", par, 4), ("st", par, 1)], w=[("st", par, 5)], cost=0.16, prio=0)
            P.add("dve", lambda e: e.tensor_scalar(out=yb[par], in0=bank(mb, 128), scalar1=st[:, 5:6], scalar2=None,
                                                   op0=ALU.mult),
                  r=[("ps", mb), ("st", par, 5)], w=[("yb", par)], cost=0.35, prio=0)
            P.add("pe", lambda e: e.transpose(out=psY(par), in_=yb[par], identity=ident),
                  r=[("yb", par), ("ident",)], w=[("ps", mb)], cost=0.2, prio=0)
            P.add("dve", lambda e: e.tensor_scalar(
                out=mixed[:, hh, qb * 128:(qb + 1) * 128], in0=psY(par),
                scalar1=gains[:, G_ATT + hh:G_ATT + hh + 1], scalar2=None, op0=ALU.mult),
                r=[("ps", mb), ("gains",)], w=[("mixed", hh, qb // 4)], cost=0.28, prio=0)

        def ffn1_post():
            norm(tiles_ext, G_MIX, sq, lnt, rstd)
            v_ones()
            proj(0)
            proj(1)
            for hh in range(8):
                for qb in range(8):
                    attn_block(hh * 8 + qb)
                if hh + 2 < 8:
                    proj(hh + 2)

        ffn(tiles_ext, GC1, HID1, pre=lambda: norm(tiles_ext, G_FFN1, sq, lnt, rstd),
            post=ffn1_post, region_open=True, serial_dma=True)
        if debug:
            P.add("sp", lambda e: e.dma_start(out=dbg["dbg_h1"].rearrange("(k p) t -> p k t", p=128), in_=h_own[:, :, :]),
                  r=[("h", k, tt) for k in range(NK) for tt in (0, 1)], dma=("dbg", 0))
        print("[sched] heads region est makespan us", getattr(P, "last_makespan", None))

        P.barrier()
        CO = 32768
        u_sb = [loc(CO + i * 4104, 4104, F32) for i in range(2)]
        m_sb = [loc(CO + 8208 + i * 4104, 4104, F32) for i in range(2)]
        y_sb = [loc(CO + 16416 + i * 2048, 2048, F32) for i in range(2)]
        yc_sb = [loc(CO + 20512 + i * 2048, 2048, F32) for i in range(2)]
        sqc = loc(CO + 24608, 1024, BF16)
        lntc = loc(CO + 25632, 2048, F32)
        rstdc = loc(CO + 27680, 2048, F32)
        assert CO + 29728 <= LOC_SIZE
        segs = [("own", 0, 512, 1), ("own", 512, 512, 513), ("edge", 255, 2, None)]
        pcnt["nb"] = 7
        P.begin_region()
        for c in range(8):
            cp = c % 2
            us, ms = u_sb[cp], m_sb[cp]
            iu, vu, ku = use_piece()
            ic, vc, kc = use_piece()
            ib, vb, kb = use_piece()
            for (reg, c0, n, m0) in segs:
                b = pbank()
                asrc = a_own if reg == "own" else a_halo
                tks = [c0 // 512] if reg == "own" else [2]
                proj_mm(vu, ku, b, n, lambda k, asrc=asrc, c0=c0, n=n: asrc[:, k, c0:c0 + n], tks)
                if reg == "own":
                    P.add("act", lambda e, b=b, m0=m0, us=us: e.copy(us[:, m0:m0 + 512], bank(b)),
                          r=[("ps", b)], w=[("u_sb", cp, m0)], cost=0.6)
                else:
                    P.add("act", lambda e, b=b, us=us: e.copy(us[:, 0:1], bank(b, 1, 0)),
                          r=[("ps", b)], w=[("u_sb", cp, 0)], cost=0.2)
                    P.add("act", lambda e, b=b, us=us: e.copy(us[:, 1025:1026], bank(b, 1, 1)),
                          r=[("ps", b)], w=[("u_sb", cp, 1025)], cost=0.2)
            release(iu)
            for (reg, c0, n, m0) in segs:
                b = pbank()
                asrc = a_own if reg == "own" else a_halo
                tks = [c0 // 512] if reg == "own" else [2]
                proj_mm(vc, kc, b, n, lambda k, asrc=asrc, c0=c0, n=n: asrc[:, k, c0:c0 + n], tks)
                if reg == "own":
                    P.add("dve", lambda e, b=b, m0=m0, us=us, ms=ms: e.tensor_tensor(
                        out=ms[:, m0:m0 + 512], in0=bank(b), in1=us[:, m0:m0 + 512], op=ALU.mult),
                        r=[("ps", b), ("u_sb", cp, m0)], w=[("m", cp, m0)], cost=0.63)
                else:
                    P.add("dve", lambda e, b=b, us=us, ms=ms: e.tensor_tensor(
                        out=ms[:, 0:1], in0=bank(b, 1, 0), in1=us[:, 0:1], op=ALU.mult),
                        r=[("ps", b), ("u_sb", cp, 0)], w=[("m", cp, 0)], cost=0.1)
                    P.add("dve", lambda e, b=b, us=us, ms=ms: e.tensor_tensor(
                        out=ms[:, 1025:1026], in0=bank(b, 1, 1), in1=us[:, 1025:1026], op=ALU.mult),
                        r=[("ps", b), ("u_sb", cp, 1025)], w=[("m", cp, 1025)], cost=0.1)
            release(ic)
            mkeys = [("m", cp, 0), ("m", cp, 1), ("m", cp, 513), ("m", cp, 1025)]
            for ti, t in enumerate(tiles_own):
                b = pbank()
                o = t[1]
                ys, ycs = y_sb[ti], yc_sb[ti]
                proj_mm(vb, kb, b, 512, lambda k, t=t: a_ap(k, t), [tkey(t)])
                P.add("dve", lambda e, o=o, c=c, ms=ms, ys=ys: e.tensor_scalar(
                    out=ys, in0=ms[:, o:o + 512], scalar1=gains[:, G_CW0 + c:G_CW0 + c + 1], scalar2=None,
                    op0=ALU.mult), r=mkeys + [("gains",)], w=[("y", ti)], cost=0.6)
                P.add("dve", lambda e, o=o, c=c, ms=ms, ys=ys: e.scalar_tensor_tensor(
                    out=ys, in0=ms[:, o + 1:o + 513], scalar=gains[:, G_CW1 + c:G_CW1 + c + 1], in1=ys,
                    op0=ALU.mult, op1=ALU.add), r=mkeys + [("y", ti)], w=[("y", ti)], cost=0.6)
                P.add("dve", lambda e, o=o, c=c, ms=ms, ys=ys: e.scalar_tensor_tensor(
                    out=ys, in0=ms[:, o + 2:o + 514], scalar=gains[:, G_CW2 + c:G_CW2 + c + 1], in1=ys,
                    op0=ALU.mult, op1=ALU.add), r=mkeys + [("y", ti)], w=[("y", ti)], cost=0.6)
                P.add("dve", lambda e, b=b, c=c, ys=ys, ycs=ycs: e.scalar_tensor_tensor(
                    out=ycs, in0=ys, scalar=gains[:, G_CB + c:G_CB + c + 1], in1=bank(b),
                    op0=ALU.add, op1=ALU.mult), r=[("y", ti), ("ps", b)], w=[("yc", ti)], cost=0.63)
                P.add("act", lambda e, ycs=ycs: e.activation(out=sqc, in_=ycs, func=AF.Square),
                      r=[("yc", ti)], w=[("sqc",)], cost=0.6)
                P.add("pe", lambda e: e.matmul(bank(7), lhsT=ones, rhs=sqc, start=True, stop=True),
                      r=[("sqc",), ("ones",)], w=[("ps", 7)], cost=0.25)
                P.add("act", lambda e: e.activation(out=lntc, in_=bank(7), func=AF.Ln, scale=1.0 / 128, bias=EPS),
                      r=[("ps", 7)], w=[("lntc",)], cost=0.7)
                P.add("act", lambda e: e.activation(out=rstdc, in_=lntc, func=AF.Exp, scale=-0.5),
                      r=[("lntc",)], w=[("rstdc",)], cost=0.7)
                P.add("dve", lambda e, o=o, c=c, ycs=ycs: e.scalar_tensor_tensor(
                    out=mixed[:, 8 + c, o:o + 512], in0=ycs, scalar=gains[:, G_CONV + c:G_CONV + c + 1],
                    in1=rstdc, op0=ALU.mult, op1=ALU.mult),
                    r=[("yc", ti), ("rstdc",), ("gains",)], w=[("mixed", 8 + c, ti)], cost=0.6)
            release(ib)

        if debug:
            P.add("pool", lambda e: e.dma_start(out=dbg["dbg_mixed"].rearrange("(k p) t -> p k t", p=128), in_=mixed[:, :, :]),
                  r=[("mixed", mc, tt) for mc in range(NK) for tt in (0, 1)], dma=("dbg", 1))

        for dc in range(NK):
            io, vo, ko = use_piece()
            for ti, t in enumerate(tiles_own):
                b = pbank()
                o = t[1]
                for q4 in range(4):
                    def mm(e, b=b, o=o, vo=vo, q4=q4):
                        for mc in range(q4 * 4, q4 * 4 + 4):
                            ins = e.matmul(bank(b), lhsT=vo[:, mc, :], rhs=mixed[:, mc, o:o + 512],
                                           start=(mc == 0), stop=(mc == NK - 1))
                        return ins
                    P.add("pe", mm, r=[ko] + [("mixed", mc, ti) for mc in range(q4 * 4, q4 * 4 + 4)], w=[("ps", b)],
                          cost=0.88)
                P.add("dve", lambda e, b=b, dc=dc, t=t: e.tensor_tensor(out=h_ap(dc, t), in0=bank(b), in1=h_ap(dc, t),
                                                                        op=ALU.add),
                      r=[("ps", b), ("h", dc, ti)], w=[("h", dc, ti)], cost=0.63)
            release(io)
        sq3 = [loc(CO + i * 1024, 1024, BF16) for i in range(2)]
        lnt3 = loc(CO + 2048, 2048, F32)
        rstd3 = [loc(CO + 4096 + i * 2048, 2048, F32) for i in range(2)]
        norm(tiles_own, G_FFN2, sq3, lnt3, rstd3)

        HID2N = 40960
        pt_bf = loc(8192, 4096, BF16).rearrange("p (k t) -> p k t", k=2)
        wple = loc(12288, 8192, BF16).rearrange("p (k f) -> p k f", k=2)
        sig = [loc(20480 + i * 2048, 2048, F32) for i in range(2)]
        P.add("pool", lambda e: e.dma_start(out=pt_bf, in_=pT.rearrange("(k p) t -> p k t", p=128)),
              r=[("a", 15, 0), ("a", 15, 1)], w=[("pt",)], dma=("misc", 2), cost=3.0)
        P.add("pool", lambda e: e.dma_start(out=wple, in_=w["ple_w_proj"].rearrange("(k p) f -> p k f", p=128)),
              r=[("a", 15, 0), ("a", 15, 1)], w=[("wple",)], dma=("misc", 3), cost=4.0)
        ffn(tiles_own, GC2, HID2N, pre=None,
            post=lambda: norm(tiles_own, G_PLE, sq2, lnt2, rstd2), region_open=True)
        print("[sched] conv+wout+ffn2-head region est makespan us", getattr(P, "last_makespan", None))
        if debug:
            P.add("sp", lambda e: e.dma_start(out=dbg["dbg_h2"].rearrange("(k p) t -> p k t", p=128), in_=h_own[:, :, :]),
                  r=[("h", k, tt) for k in range(NK) for tt in (0, 1)], dma=("dbg", 2))
        if debug:
            P.add("sp", lambda e: e.dma_start(out=dbg["dbg_h3"].rearrange("(k p) t -> p k t", p=128), in_=h_own[:, :, :]),
                  r=[("h", k, tt) for k in range(NK) for tt in (0, 1)], dma=("dbg", 3))

        P.begin_region()
        cc = 0
        for dc in range(NK):
            ig, vg, kg = use_piece()
            for ti, t in enumerate(tiles_own):
                o = t[1]
                bA, bB = cc % 2, 2 + cc % 2
                si = cc % 2
                cc += 1
                for q4 in range(4):
                    def mm(e, bA=bA, t=t, vg=vg, q4=q4):
                        for k in range(q4 * 4, q4 * 4 + 4):
                            ins = e.matmul(bank(bA), lhsT=vg[:, k, :], rhs=a_ap(k, t), start=(k == 0), stop=(k == NK - 1))
                        return ins
                    P.add("pe", mm, r=[kg] + [("a", k, ti) for k in range(q4 * 4, q4 * 4 + 4)], w=[("ps", bA)], cost=0.88)

                def mm2(e, bB=bB, o=o, dc=dc):
                    for k2 in range(2):
                        ins = e.matmul(bank(bB), lhsT=wple[:, k2, dc * 128:(dc + 1) * 128], rhs=pt_bf[:, k2, o:o + 512],
                                       start=(k2 == 0), stop=(k2 == 1))
                    return ins
                P.add("pe", mm2, r=[("wple",), ("pt",)], w=[("ps", bB)], cost=0.45)
                P.add("act", lambda e, bA=bA, si=si: e.activation(out=sig[si], in_=bank(bA), func=AF.Sigmoid),
                      r=[("ps", bA)], w=[("sig", si)], cost=0.62)
                P.add("dve", lambda e, bB=bB, si=si: e.tensor_tensor(out=sig[si], in0=bank(bB), in1=sig[si], op=ALU.mult),
                      r=[("ps", bB), ("sig", si)], w=[("sig", si)], cost=0.62)
                P.add("dve", lambda e, si=si, dc=dc, t=t: e.tensor_tensor(out=h_ap(dc, t), in0=sig[si], in1=h_ap(dc, t),
                                                                         op=ALU.add),
                      r=[("sig", si), ("h", dc, ti)], w=[("h", dc, ti)], cost=0.62)
            release(ig)
        norm(tiles_own, G_FIN, sq2, lnt2, rstd2, out_fn=lambda k, t: h_ap(k, t),
             out_keys=lambda k, t: [("h", k, tkey(t))])
        outv = outT.rearrange("(k p) t -> p k t", p=128)
        for ti, t in enumerate(tiles_own):
            o = t[1]
            for kh in range(2):
                P.add("sp", lambda e, o=o, kh=kh: e.dma_start(out=outv[:, kh * 8:(kh + 1) * 8, o:o + 512],
                                                             in_=h_own[:, kh * 8:(kh + 1) * 8, o:o + 512]),
                      r=[("h", k, ti) for k in range(kh * 8, kh * 8 + 8)], dma=("out", ti * 2 + kh), cost=3.0)
        P.end_region()

        assert state["next_use"] == len(pieces), (state, len(pieces))

        P.resolve()
        dma_names = sorted(set(op.dma[0] for op in P.ops if op.dma is not None), key=str)
        from contextlib import ExitStack
        with ExitStack() as es:
            sems = {}
            for e_ in ("pe", "act", "dve"):
                sems[("eng", e_)] = es.enter_context(nc.semaphore("s_" + e_))
            for dn in dma_names:
                sems[("dma", dn)] = es.enter_context(nc.semaphore("d_%s_%s" % (dn[0], dn[1])))
            block = es.enter_context(nc.Block())
            by_eng = {}
            for op in P.ops:
                by_eng.setdefault(op.eng, []).append(op)

            def runner(name):
                def f(e):
                    for op in by_eng.get(name, []):
                        for (k, v) in op.waits:
                            e.wait_ge(sems[k], v)
                        if op.fn is None:
                            continue
                        ins = op.fn(e)
                        if op.dma is not None:
                            ins.then_inc(sems[("dma", op.dma[0])], 16)
                        elif op.milestone:
                            ins.then_inc(sems[("eng", name)], 1)
                    if name == "sp":
                        for dn, cntv in P.dma_counts.items():
                            if dn[0] in ("out", "dbg"):
                                e.wait_ge(sems[("dma", dn)], cntv)
                return f
            block.sync(runner("sp"))
            block.gpsimd(runner("pool"))
            block.tensor(runner("pe"))
            block.scalar(runner("act"))
            block.vector(runner("dve"))
    return nc


def _tables(rpb, j):
    rpb = np.asarray(rpb, np.float32)
    out = np.full((5, 8, 128, 768), -1e30, np.float32)
    qi = np.arange(128)
    qr, qc = qi // 64, qi % 64
    cs = np.clip(qc - 8, 0, 48)
    for ti, qb in enumerate((0, 1, 2, 6, 7)):
        lo, nkb, nk = _geom(qb)
        idx = np.arange(nk)
        e = lo * 2 + idx // 64
        kc = idx % 64
        kr = 16 * j - 4 + e
        r = 16 * j + 2 * qb + qr
        rs = np.clip(r - 4, 0, 56)
        valid = ((kr[None, :] >= rs[:, None]) & (kr[None, :] < rs[:, None] + 8)
                 & (kr[None, :] >= 0) & (kr[None, :] < 64)
                 & (kc[None, :] >= cs[:, None]) & (kc[None, :] < cs[:, None] + 16))
        rr = np.clip(kr[None, :] - r[:, None] + 7, 0, 14)
        rc = np.clip(kc[None, :] - qc[:, None] + 15, 0, 30)
        for hh in range(8):
            vals = rpb[hh][rr, rc]
            out[ti, hh, :, :nk] = np.where(valid, vals, np.float32(-1e30))
    return out


def _pm(v):
    v = np.asarray(v, np.float32).reshape(-1, 128)
    return np.ascontiguousarray(v.T)


_NC_CACHE = {}


def _prepare(inputs):
    x = np.asarray(inputs["x"], np.float32)
    p = np.asarray(inputs["p"], np.float32)[0]
    gains = np.concatenate([
        _pm(inputs["ffn1_norm"][0]), _pm(inputs["mix_norm"][0]), _pm(inputs["ffn2_norm"][0]),
        _pm(inputs["ple_norm"][0]), _pm(inputs["final_norm"]),
        _pm(inputs["attn_out_norm"][0]), _pm(inputs["conv_out_norm"][0]),
        _pm(inputs["conv_w"][0][0]), _pm(inputs["conv_w"][0][1]), _pm(inputs["conv_w"][0][2]),
        _pm(inputs["conv_b"][0])], axis=1)
    assert gains.shape == (128, 128)
    gains = np.ascontiguousarray(gains, np.float32)
    ident = np.eye(128, dtype=np.float32)
    shared = {nm: np.ascontiguousarray(np.asarray(inputs[nm], np.float32)[0]) for nm in
              ("ffn1_wg", "ffn1_wu", "ffn1_wd", "w_in", "w_out", "ffn2_wg", "ffn2_wu", "ffn2_wd",
               "ple_w_gate", "ple_w_proj")}
    tabs = [_tables(inputs["rpb"][0], j) for j in range(4)]
    in_maps = []
    for c in range(8):
        b, j = divmod(c, 4)
        t0 = 1024 * j
        xb = x[b]
        xo = np.ascontiguousarray(xb[t0:t0 + 1024].T)
        xh = np.zeros((HALO, D), np.float32)
        lo = t0 - 256
        if lo >= 0:
            xh[0:256] = xb[lo:t0]
        hi = t0 + 1024
        if hi + 192 <= 4096:
            xh[256:448] = xb[hi:hi + 192]
        xh = np.ascontiguousarray(xh.T)
        pTc = np.ascontiguousarray(p[b, t0:t0 + 1024].T)
        m = {"xo": xo, "xh": xh, "pT": pTc, "gains": gains, "ident": ident, "tbl": tabs[j]}
        m.update(shared)
        in_maps.append(m)
    return in_maps


def kernel(**inputs):
    in_maps = _prepare(inputs)
    if "nc" not in _NC_CACHE:
        _NC_CACHE["nc"] = build_program()
    nc = _NC_CACHE["nc"]
    res = run_bass_kernel_spmd(nc, in_maps, core_ids=list(range(8)))
    out = np.empty((2, 4096, D), np.float32)
    for c in range(8):
        b, j = divmod(c, 4)
        out[b, 1024 * j:1024 * (j + 1)] = res.results[c]["outT"].T
    return out
```
